# Optimizing a Trainium2 kernel written in Bass

```python
import math
import jax, jax.numpy as jnp
from jax import lax
import numpy as np

D_MODEL = 4096
BATCH = 2
SEQ = 4096
DEPTH = 2

HEAD_DIM = 128
DIFF_HEADS = 6
DIFF_DV = 2 * HEAD_DIM
MOBA_HEADS = 10
DSA_HEADS = 10
MIX_WIDTH = DIFF_HEADS * DIFF_DV + MOBA_HEADS * HEAD_DIM + DSA_HEADS * HEAD_DIM
MOBA_BLOCK = 256
MOBA_TOPK = 3
MOBA_Q_CHUNK = 32
DSA_TOPK = 256
KV_LATENT = 512
IDX_HEADS = 32
IDX_DIM = 64
DSA_Q_CHUNK = 128
DENSE_Q_BLOCK = 128
NUM_BUCKETS = 32
MAX_DISTANCE = 128
N_BIAS_COLS = 2 * DIFF_HEADS + MOBA_HEADS + DSA_HEADS
D_FF = 4 * D_MODEL
ALPHA = (2.0 * DEPTH) ** 0.25
BETA = (8.0 * DEPTH) ** -0.25
LN_EPS = 1e-5
RMS_EPS = 1e-5
NEG_INF = -1e30
IN_SIZES = (2 * DIFF_HEADS * HEAD_DIM,
            2 * DIFF_HEADS * HEAD_DIM,
            DIFF_HEADS * DIFF_DV,
            MOBA_HEADS * HEAD_DIM,
            MOBA_HEADS * HEAD_DIM,
            MOBA_HEADS * HEAD_DIM,
            DSA_HEADS * HEAD_DIM,
            KV_LATENT,
            IDX_HEADS * IDX_DIM,
            IDX_DIM,
            IDX_HEADS)
D_IN = sum(IN_SIZES)

kernel_name = 'hymba_diff_moba_dsa_deepnorm'


def _layer_norm(x, g, b):
    xf = x.astype(jnp.float32)
    mu = jnp.mean(xf, axis=-1, keepdims=True)
    var = jnp.mean(jnp.square(xf - mu), axis=-1, keepdims=True)
    y = (xf - mu) * lax.rsqrt(var + LN_EPS)
    return (y * g.astype(jnp.float32) + b.astype(jnp.float32)).astype(x.dtype)


def _standardize(x):
    xf = x.astype(jnp.float32)
    mu = jnp.mean(xf, axis=-1, keepdims=True)
    var = jnp.mean(jnp.square(xf - mu), axis=-1, keepdims=True)
    return ((xf - mu) * lax.rsqrt(var + LN_EPS)).astype(x.dtype)


def _rms_norm(x, g):
    xf = x.astype(jnp.float32)
    y = xf * lax.rsqrt(jnp.mean(jnp.square(xf), axis=-1, keepdims=True) + RMS_EPS)
    return (y * g.astype(jnp.float32)).astype(x.dtype)


def _t5_bucket(dist):
    n = jnp.maximum(dist, 0)
    max_exact = NUM_BUCKETS // 2
    nf = jnp.maximum(n, 1).astype(jnp.float32)
    large = max_exact + (jnp.log(nf / max_exact) / math.log(MAX_DISTANCE / max_exact)
                         * (NUM_BUCKETS - max_exact)).astype(jnp.int32)
    large = jnp.minimum(large, NUM_BUCKETS - 1)
    return jnp.where(n < max_exact, n, large)


def diff_attention(q, k, v, lam_vecs, subln_g, tab, lambda_init):
    B, T = q.shape[0], q.shape[1]
    qh = q.transpose(0, 2, 3, 1, 4)
    kh = k.transpose(0, 2, 3, 1, 4)
    vh = v.transpose(0, 2, 1, 3)
    lv = lam_vecs.astype(jnp.float32)
    lam = jnp.exp(jnp.sum(lv[0] * lv[1])) - jnp.exp(jnp.sum(lv[2] * lv[3])) + lambda_init
    key_pos = jnp.arange(T)
    scale = HEAD_DIM ** -0.5

    def block(i):
        q0 = i * DENSE_Q_BLOCK
        qb = lax.dynamic_slice_in_dim(qh, q0, DENSE_Q_BLOCK, axis=3)
        s = jnp.einsum('bhmqd,bhmkd->bhmqk', qb, kh).astype(jnp.float32) * scale
        dist = (q0 + jnp.arange(DENSE_Q_BLOCK))[:, None] - key_pos[None, :]
        bias = tab[_t5_bucket(dist)].reshape(DENSE_Q_BLOCK, T, DIFF_HEADS, 2).transpose(2, 3, 0, 1)
        s = jnp.where(dist >= 0, s + bias.astype(jnp.float32), NEG_INF)
        p = jax.nn.softmax(s, axis=-1)
        a = p[:, :, 0] - lam * p[:, :, 1]
        return jnp.einsum('bhqk,bhkd->bhqd', a.astype(vh.dtype), vh)

    o = lax.map(block, jnp.arange(T // DENSE_Q_BLOCK))
    o = o.transpose(1, 0, 3, 2, 4).reshape(B, T, DIFF_HEADS, DIFF_DV)
    o = _rms_norm(o, subln_g) * (1.0 - lambda_init)
    return o.reshape(B, T, DIFF_HEADS * DIFF_DV)


def moba_attention(q, k, v, tab):
    B, T, H, d = q.shape
    nb = max(-(-T // MOBA_BLOCK), MOBA_TOPK)
    Lp = nb * MOBA_BLOCK
    pad = ((0, 0), (0, 0), (0, Lp - T), (0, 0))
    qh = q.transpose(0, 2, 1, 3)
    kh = jnp.pad(k.transpose(0, 2, 1, 3), pad)
    vh = jnp.pad(v.transpose(0, 2, 1, 3), pad)
    k_blocks = kh.reshape(B, H, nb, MOBA_BLOCK, d)
    v_blocks = vh.reshape(B, H, nb, MOBA_BLOCK, d)
    k_mean = jnp.mean(k_blocks.astype(jnp.float32), axis=3)
    bi = jnp.arange(B)[:, None, None, None]
    hi = jnp.arange(H)[None, :, None, None]
    hi5 = jnp.arange(H)[None, :, None, None, None]
    tab_t = tab.T
    scale = d ** -0.5

    def chunk(i):
        q0 = i * MOBA_Q_CHUNK
        t = q0 + jnp.arange(MOBA_Q_CHUNK)
        own = q0 // MOBA_BLOCK
        qc = lax.dynamic_slice_in_dim(qh, q0, MOBA_Q_CHUNK, axis=2)
        gate = jnp.einsum('bhqd,bhnd->bhqn', qc.astype(jnp.float32), k_mean)
        gate = jnp.where(jnp.arange(nb) < own, gate, NEG_INF)
        _, sel = lax.top_k(gate, MOBA_TOPK)
        valid = jnp.arange(MOBA_TOPK) < own
        ks = k_blocks[bi, hi, sel]
        vs = v_blocks[bi, hi, sel]
        s_sel = jnp.einsum('bhqd,bhqnkd->bhqnk', qc, ks).astype(jnp.float32) * scale
        pos_sel = sel[..., None] * MOBA_BLOCK + jnp.arange(MOBA_BLOCK)
        bias_sel = tab_t[hi5, _t5_bucket(t[None, None, :, None, None] - pos_sel)]
        s_sel = jnp.where(valid[:, None], s_sel + bias_sel.astype(jnp.float32), NEG_INF)
        k_own = lax.dynamic_slice_in_dim(kh, own * MOBA_BLOCK, MOBA_BLOCK, axis=2)
        v_own = lax.dynamic_slice_in_dim(vh, own * MOBA_BLOCK, MOBA_BLOCK, axis=2)
        s_own = jnp.einsum('bhqd,bhkd->bhqk', qc, k_own).astype(jnp.float32) * scale
        dist_own = t[:, None] - (own * MOBA_BLOCK + jnp.arange(MOBA_BLOCK))[None, :]
        bias_own = tab[_t5_bucket(dist_own)].transpose(2, 0, 1)
        s_own = jnp.where(dist_own >= 0, s_own + bias_own.astype(jnp.float32), NEG_INF)
        logits = jnp.concatenate([s_sel.reshape(B, H, MOBA_Q_CHUNK, MOBA_TOPK * MOBA_BLOCK), s_own], axis=-1)
        p = jax.nn.softmax(logits, axis=-1).astype(v.dtype)
        p_sel = p[..., :MOBA_TOPK * MOBA_BLOCK].reshape(B, H, MOBA_Q_CHUNK, MOBA_TOPK, MOBA_BLOCK)
        p_own = p[..., MOBA_TOPK * MOBA_BLOCK:]
        return (jnp.einsum('bhqnk,bhqnkd->bhqd', p_sel, vs)
                + jnp.einsum('bhqk,bhkd->bhqd', p_own, v_own))

    o = lax.map(chunk, jnp.arange(T // MOBA_Q_CHUNK))
    return o.transpose(1, 0, 3, 2, 4).reshape(B, T, H * d)


def dsa_attention(q, c_kv, q_idx, k_idx, w_idx, w_uk, w_uv, tab):
    B, T, H, d = q.shape
    n_top = min(DSA_TOPK, T // 4)
    key_pos = jnp.arange(T)
    bi = jnp.arange(B)[:, None, None]
    scale = d ** -0.5

    def chunk(i):
        q0 = i * DSA_Q_CHUNK
        t = q0 + jnp.arange(DSA_Q_CHUNK)
        qi = lax.dynamic_slice_in_dim(q_idx, q0, DSA_Q_CHUNK, axis=1)
        wi = lax.dynamic_slice_in_dim(w_idx, q0, DSA_Q_CHUNK, axis=1)
        qc = lax.dynamic_slice_in_dim(q, q0, DSA_Q_CHUNK, axis=1)
        rel = jax.nn.relu(jnp.einsum('bqhe,bse->bqhs', qi, k_idx).astype(jnp.float32) * IDX_DIM ** -0.5)
        score = jnp.einsum('bqhs,bqh->bqs', rel, wi.astype(jnp.float32))
        score = jnp.where(key_pos[None, :] <= t[:, None], score, NEG_INF)
        _, sel = lax.top_k(score, n_top)
        valid = jnp.arange(n_top)[None, :] < (t + 1)[:, None]
        c_sel = c_kv[bi, sel]
        q_lat = jnp.einsum('bqhd,hcd->bqhc', qc, w_uk)
        s = jnp.einsum('bqhc,bqkc->bhqk', q_lat, c_sel).astype(jnp.float32) * scale
        bias = tab[_t5_bucket(t[None, :, None] - sel)].transpose(0, 3, 1, 2)
        s = jnp.where(valid, s + bias.astype(jnp.float32), NEG_INF)
        p = jax.nn.softmax(s, axis=-1).astype(c_kv.dtype)
        o_lat = jnp.einsum('bhqk,bqkc->bqhc', p, c_sel)
        return jnp.einsum('bqhc,hcd->bqhd', o_lat, w_uv)

    o = lax.map(chunk, jnp.arange(T // DSA_Q_CHUNK))
    return o.transpose(1, 0, 2, 3, 4).reshape(B, T, H * d)


def hybrid_layer(x, rel_bias, w_in, diff_lambda, diff_subln_g, kv_norm_g, w_uk, w_uv, w_o,
                 ln1_g, ln1_b, w_up, w_down, ln2_g, ln2_b, layer_idx):
    B, T, _ = x.shape
    proj = x @ w_in
    offs = np.cumsum(IN_SIZES)[:-1].tolist()
    dq, dk, dv, mq, mk, mv, cq, ckv, iq, ik, iw = jnp.split(proj, offs, axis=-1)
    lambda_init = 0.8 - 0.6 * math.exp(-0.3 * layer_idx)
    c0 = 2 * DIFF_HEADS
    c1 = c0 + MOBA_HEADS
    y_diff = diff_attention(dq.reshape(B, T, DIFF_HEADS, 2, HEAD_DIM),
                            dk.reshape(B, T, DIFF_HEADS, 2, HEAD_DIM),
                            dv.reshape(B, T, DIFF_HEADS, DIFF_DV),
                            diff_lambda, diff_subln_g, rel_bias[:, :c0], lambda_init)
    y_moba = moba_attention(mq.reshape(B, T, MOBA_HEADS, HEAD_DIM),
                            mk.reshape(B, T, MOBA_HEADS, HEAD_DIM),
                            mv.reshape(B, T, MOBA_HEADS, HEAD_DIM), rel_bias[:, c0:c1])
    y_dsa = dsa_attention(cq.reshape(B, T, DSA_HEADS, HEAD_DIM), _rms_norm(ckv, kv_norm_g),
                          iq.reshape(B, T, IDX_HEADS, IDX_DIM), _standardize(ik),
                          iw * IDX_HEADS ** -0.5, w_uk, w_uv, rel_bias[:, c1:])
    y = jnp.concatenate([y_diff, y_moba, y_dsa], axis=-1) @ w_o
    x = _layer_norm(ALPHA * x + y, ln1_g, ln1_b)
    h = jax.nn.relu(x @ w_up)
    x = _layer_norm(ALPHA * x + (h * h) @ w_down, ln2_g, ln2_b)
    return x


def setup_inputs(seed: int = 0) -> dict:
    key = jax.random.key(seed)
    ks = jax.random.split(key, 17)
    f32 = jnp.float32
    nrm = lambda k, shape: jax.random.normal(k, shape, f32)
    return {
        'x': nrm(ks[0], (BATCH, SEQ, D_MODEL)),
        'ln_emb_g': 1.0 + 0.02 * nrm(ks[1], (D_MODEL,)),
        'ln_emb_b': 0.02 * nrm(ks[2], (D_MODEL,)),
        'rel_bias': 0.2 * nrm(ks[3], (NUM_BUCKETS, N_BIAS_COLS)),
        'w_in': nrm(ks[4], (DEPTH, D_MODEL, D_IN)) * D_MODEL ** -0.5,
        'diff_lambda': 0.1 * nrm(ks[5], (DEPTH, 4, HEAD_DIM)),
        'diff_subln_g': 1.0 + 0.02 * nrm(ks[6], (DEPTH, DIFF_DV)),
        'kv_norm_g': 1.0 + 0.02 * nrm(ks[7], (DEPTH, KV_LATENT)),
        'w_uk': nrm(ks[8], (DEPTH, DSA_HEADS, KV_LATENT, HEAD_DIM)) * KV_LATENT ** -0.5,
        'w_uv': nrm(ks[9], (DEPTH, DSA_HEADS, KV_LATENT, HEAD_DIM)) * KV_LATENT ** -0.5,
        'w_o': nrm(ks[10], (DEPTH, MIX_WIDTH, D_MODEL)) * (MIX_WIDTH ** -0.5 * BETA),
        'ln1_g': 1.0 + 0.02 * nrm(ks[11], (DEPTH, D_MODEL)),
        'ln1_b': 0.02 * nrm(ks[12], (DEPTH, D_MODEL)),
        'w_up': nrm(ks[13], (DEPTH, D_MODEL, D_FF)) * D_MODEL ** -0.5,
        'w_down': nrm(ks[14], (DEPTH, D_FF, D_MODEL)) * (D_FF ** -0.5 * BETA),
        'ln2_g': 1.0 + 0.02 * nrm(ks[15], (DEPTH, D_MODEL)),
        'ln2_b': 0.02 * nrm(ks[16], (DEPTH, D_MODEL)),
    }


def reference(x, ln_emb_g, ln_emb_b, rel_bias, w_in, diff_lambda, diff_subln_g, kv_norm_g,
              w_uk, w_uv, w_o, ln1_g, ln1_b, w_up, w_down, ln2_g, ln2_b):
    h = _layer_norm(x, ln_emb_g, ln_emb_b)
    for l in range(DEPTH):
        h = hybrid_layer(h, rel_bias, w_in[l], diff_lambda[l], diff_subln_g[l], kv_norm_g[l],
                         w_uk[l], w_uv[l], w_o[l], ln1_g[l], ln1_b[l], w_up[l], w_down[l],
                         ln2_g[l], ln2_b[l], l)
    return h
```

```python
import math
import numpy as np
import ml_dtypes
from contextlib import ExitStack
import concourse.bass as bass
import concourse.mybir as mybir
from concourse.bass_utils import run_bass_kernel_spmd

F32 = mybir.dt.float32
BF16 = mybir.dt.bfloat16
AF = mybir.ActivationFunctionType
ALU = mybir.AluOpType
AX = mybir.AxisListType
NPBF = ml_dtypes.bfloat16

D = 4096
T = 4096
NB = 2
DEPTH = 2
DIN = 12384
DFF = 16384
ALPHA = (2.0 * DEPTH) ** 0.25
LN_EPS = 1e-5
NEGM = -32768.0
SCALE = 128 ** -0.5
O_DQ, O_DK, O_DV, O_MQ, O_MK, O_MV, O_CQ, O_CKV, O_IQ, O_IK, O_IW = (
    0, 1536, 3072, 4608, 5888, 7168, 8448, 9728, 10240, 12288, 12352)

SAME_ENGINE_SYNC = True


class Buf:
    __slots__ = ("name", "w", "r")

    def __init__(self, name=""):
        self.name = name
        self.w = None
        self.r = []


class K:
    def __init__(self, nc, es):
        self.nc = nc
        self.es = es
        self.eng = {"pe": nc.tensor, "act": nc.scalar, "dve": nc.vector, "pool": nc.gpsimd, "sp": nc.sync}
        self.sems = {}
        self.cnt = {}
        self.cur = {}
        self.waited = {e: {} for e in self.eng}
        self.nsem = 0
        for e in self.eng:
            self.new_phase_sem(e)

    def _mk_sem(self, key):
        h = self.es.enter_context(self.nc.semaphore("s%d" % self.nsem))
        self.nsem += 1
        self.sems[key] = h
        self.cnt[key] = 0
        return h

    def new_phase_sem(self, e):
        key = "%s#%d" % (e, self.nsem)
        self._mk_sem(key)
        self.cur[e] = key

    def dma_sem(self, name):
        key = "dma_" + name
        if key not in self.sems:
            self._mk_sem(key)
        return key

    def _deps(self, e, reads, writes):
        deps = {}

        def add(d):
            if d is None:
                return
            sk, c, de = d
            if de == e and (e == "pe" or not SAME_ENGINE_SYNC):
                return
            if de == "dma":
                c = self.cnt[sk]
            if deps.get(sk, 0) < c:
                deps[sk] = c

        for b in reads:
            add(b.w)
        for b in writes:
            add(b.w)
            for d in b.r:
                add(d)
        out = []
        wd = self.waited[e]
        for sk, c in deps.items():
            if wd.get(sk, 0) >= c:
                continue
            wd[sk] = c
            out.append((sk, c))
        return out

    def _record(self, tag, reads, writes):
        for b in reads:
            b.r.append(tag)
            if len(b.r) > 64:
                b.r = b.r[-48:]
        for b in writes:
            b.w = tag
            b.r = []

    def _bump(self, e):
        sk = self.cur[e]
        self.cnt[sk] += 1
        tag = (sk, self.cnt[sk], e)
        if self.cnt[sk] >= 30000:
            self.new_phase_sem(e)
        return sk, tag

    def op(self, e, fn, reads=(), writes=()):
        deps = self._deps(e, reads, writes)
        eng = self.eng[e]
        for sk, c in deps[1:]:
            eng.wait_ge(self.sems[sk], c)
        ins = fn(eng)
        if deps:
            ins._wait_ge(self.sems[deps[0][0]], deps[0][1])
        sk, tag = self._bump(e)
        ins.then_inc(self.sems[sk], 1)
        self._record(tag, reads, writes)
        return ins

    def pe_group(self, fns, reads=(), writes=()):
        e = "pe"
        deps = self._deps(e, reads, writes)
        eng = self.eng[e]
        for sk, c in deps[1:]:
            eng.wait_ge(self.sems[sk], c)
        ins = None
        for i, fn in enumerate(fns):
            ins = fn(eng)
            if i == 0 and deps:
                ins._wait_ge(self.sems[deps[0][0]], deps[0][1])
        sk, tag = self._bump(e)
        ins.then_inc(self.sems[sk], 1)
        self._record(tag, reads, writes)
        return ins

    def dma(self, q, out, in_, reads=(), writes=(), sem="d"):
        deps = self._deps(q, reads, writes)
        eng = self.eng[q]
        for sk, c in deps:
            eng.wait_ge(self.sems[sk], c)
        sk = self.dma_sem(sem)
        ins = eng.dma_start(out=out, in_=in_)
        self.cnt[sk] += 16
        ins.then_inc(self.sems[sk], 16)
        self._record((sk, self.cnt[sk], "dma"), reads, writes)
        return ins

    def finish(self):
        for sk, c in self.cnt.items():
            if sk.startswith("dma_") and c > 0:
                self.eng["sp"].wait_ge(self.sems[sk], c)


class Prog:
    def __init__(self):
        self.nc = bass.Bass("TRN2", target_bir_lowering=False)
        self.es = ExitStack()
        self.k = K(self.nc, self.es)
        self.ins = []
        self.outs = []
        self.ps = []
        self.ps_i = 0
        self._st_i = 0

    def din(self, name, shape, dt):
        self.ins.append(name)
        return self.nc.dram_tensor(name, list(shape), dt, kind="ExternalInput").ap()

    def dout(self, name, shape, dt):
        self.outs.append(name)
        return self.nc.dram_tensor(name, list(shape), dt, kind="ExternalOutput").ap()

    def sb(self, name, shape, dt):
        return self.es.enter_context(self.nc.sbuf_tensor(name, list(shape), dt))

    def mk_psum(self):
        for i in range(8):
            t = self.es.enter_context(self.nc.psum_tensor("ps%d" % i, [128, 512], F32))
            self.ps.append((t, Buf("ps%d" % i)))

    def ps_next(self, lo=0, hi=8):
        i = lo + (self.ps_i % (hi - lo))
        self.ps_i += 1
        return self.ps[i]

    def mk_consts(self, ident_d, ones_d):
        k = self.k
        self.identf = self.sb("identf", [128, 128], F32)
        self.identb = self.sb("identb", [128, 128], BF16)
        self.onesf = self.sb("onesf", [128, 128], F32)
        self.onesb = self.sb("onesb", [128, 128], BF16)
        self.C = Buf("consts")
        k.dma("sp", self.identf[:], ident_d[:, :], writes=[self.C], sem="c0")
        k.dma("sp", self.onesf[:], ones_d[:, :], writes=[self.C], sem="c0")
        k.op("dve", lambda e: e.tensor_copy(out=self.identb[:], in_=self.identf[:]), reads=[self.C], writes=[self.C])
        k.op("dve", lambda e: e.tensor_copy(out=self.onesb[:], in_=self.onesf[:]), reads=[self.C], writes=[self.C])

    def mk_stage(self, n=4):
        self.stg = [(self.sb("stg%d" % i, [128, 512], BF16), Buf("stg%d" % i)) for i in range(n)]
        self.stgf = [(self.sb("stgf%d" % i, [128, 512], F32), Buf("stgf%d" % i)) for i in range(3)]
        self._stf_i = 0

    def stage(self):
        i = self._st_i % len(self.stg)
        self._st_i += 1
        return self.stg[i] + ("st%d" % i,)

    def stagef(self):
        i = self._stf_i % len(self.stgf)
        self._stf_i += 1
        return self.stgf[i] + ("stf%d" % i,)

    def mk_wslots(self, n):
        self.wsl = [(self.sb("wsl%d" % i, [128, 8192], BF16), Buf("wsl%d" % i)) for i in range(n)]
        self._w_i = 0

    def w_next(self):
        i = self._w_i % len(self.wsl)
        self._w_i += 1
        return self.wsl[i] + ("w%d" % i,)


def dense_fm(p, wview, n0, ncols, xb, XB, evac):
    k = p.k
    for g0 in range(0, ncols, 256):
        w = min(256, ncols - g0)
        slot, SL, sname = p.w_next()
        sv = slot[:, 0:32 * w].rearrange("p (a b) -> p a b", b=w)
        k.dma("pool", sv, wview[:, :, n0 + g0:n0 + g0 + w], writes=[SL], sem=sname)
        for c in range(0, w, 128):
            ps, PS = p.ps_next()
            k.pe_group([(lambda e, kc=kc, c=c, ps=ps, sv=sv: e.matmul(ps[:], lhsT=sv[:, kc, c:c + 128], rhs=xb[:, kc, :],
                                                                      start=(kc == 0), stop=(kc == 31)))
                        for kc in range(32)], reads=[SL, XB], writes=[PS])
            evac((g0 + c) // 128, ps, PS)


def dense_tm(p, wview, n0, ncols, xb, XB, evac):
    k = p.k
    for g0 in range(0, ncols, 256):
        w = min(256, ncols - g0)
        slot, SL, sname = p.w_next()
        sv = slot[:, 0:32 * w].rearrange("p (a b) -> p a b", b=w)
        k.dma("pool", sv, wview[:, :, n0 + g0:n0 + g0 + w], writes=[SL], sem=sname)
        for tb in range(4):
            ps, PS = p.ps_next()
            k.pe_group([(lambda e, kc=kc, tb=tb, ps=ps, sv=sv, w=w: e.matmul(ps[:, 0:w], lhsT=xb[:, kc, tb * 128:(tb + 1) * 128],
                                                                             rhs=sv[:, kc, 0:w], start=(kc == 0), stop=(kc == 31)))
                        for kc in range(32)], reads=[SL, XB], writes=[PS])
            evac(tb, g0, w, ps, PS)


def layernorm_fm(p, acc, ACC, xb, XB, gcol, bcol, GB):
    k = p.k
    psm, PSM = p.ps[6]
    psq, PSQ = p.ps[7]
    k.pe_group([(lambda e, kc=kc: e.matmul(psm[:], lhsT=p.onesf[:], rhs=acc[:, kc, :], start=(kc == 0), stop=(kc == 31)))
                for kc in range(32)], reads=[ACC, p.C], writes=[PSM])
    for kc in range(32):
        sq, SQ, _ = p.stagef()
        k.op("act", lambda e, kc=kc, sq=sq: e.activation(out=sq[:], in_=acc[:, kc, :], func=AF.Square), reads=[ACC], writes=[SQ])
        k.pe_group([lambda e, kc=kc, sq=sq: e.matmul(psq[:], lhsT=p.onesf[:], rhs=sq[:], start=(kc == 0), stop=(kc == 31))],
                   reads=[SQ, p.C], writes=[PSQ])
    mu, rstd, MS = p.ln_mu, p.ln_rstd, p.LNS
    k.op("dve", lambda e: e.tensor_scalar(out=mu[:], in0=psm[:], scalar1=1.0 / D, scalar2=None, op0=ALU.mult), reads=[PSM], writes=[MS])
    k.op("dve", lambda e: e.tensor_tensor(out=rstd[:], in0=mu[:], in1=mu[:], op=ALU.mult), reads=[MS], writes=[MS])
    k.op("dve", lambda e: e.scalar_tensor_tensor(out=rstd[:], in0=psq[:], scalar=1.0 / D, in1=rstd[:], op0=ALU.mult, op1=ALU.subtract),
         reads=[PSQ, MS], writes=[MS])
    k.op("dve", lambda e: e.tensor_scalar(out=rstd[:], in0=rstd[:], scalar1=LN_EPS, scalar2=None, op0=ALU.add), reads=[MS], writes=[MS])
    k.op("act", lambda e: e.activation(out=rstd[:], in_=rstd[:], func=AF.Sqrt), reads=[MS], writes=[MS])
    k.op("dve", lambda e: e.reciprocal(out=rstd[:], in_=rstd[:]), reads=[MS], writes=[MS])
    for kc in range(32):
        k.op("dve", lambda e, kc=kc: e.tensor_tensor(out=acc[:, kc, :], in0=acc[:, kc, :], in1=mu[:], op=ALU.subtract), reads=[ACC, MS], writes=[ACC])
        k.op("dve", lambda e, kc=kc: e.tensor_tensor(out=acc[:, kc, :], in0=acc[:, kc, :], in1=rstd[:], op=ALU.mult), reads=[ACC, MS], writes=[ACC])
        k.op("act", lambda e, kc=kc: e.activation(out=acc[:, kc, :], in_=acc[:, kc, :], func=AF.Identity,
                                                  bias=bcol[:, kc:kc + 1], scale=gcol[:, kc:kc + 1]), reads=[ACC, GB], writes=[ACC])
        k.op("act", lambda e, kc=kc: e.activation(out=xb[:, kc, :], in_=acc[:, kc, :], func=AF.Copy), reads=[ACC], writes=[XB])


def load_cols(p, name, dram, ncol):
    t = p.sb(name, [128, ncol], F32)
    B = Buf(name)
    p.k.dma("sp", t[:], dram[:, :], writes=[B], sem="c0")
    return t, B


def declare_proj_outputs(p):
    o = {}
    o["qd"] = p.dout("qd", [2, 12, 128, 512], BF16)
    o["kd"] = p.dout("kd", [2, 12, 128, 512], BF16)
    o["vd"] = p.dout("vd", [2, 4, 128, 1536], BF16)
    o["qm"] = p.dout("qm", [2, 10, 128, 512], BF16)
    o["km"] = p.dout("km", [2, 10, 128, 512], BF16)
    o["vm"] = p.dout("vm", [2, 4, 128, 1280], BF16)
    o["ql"] = p.dout("ql", [2, 40, 128, 512], BF16)
    o["kvT"] = p.dout("kvT", [2, 4, 128, 512], BF16)
    o["kv"] = p.dout("kv", [2, 4, 128, 512], BF16)
    o["qi"] = p.dout("qi", [2, 16, 128, 512], BF16)
    o["ikT"] = p.dout("ikT", [2, 64, 512], BF16)
    o["iw"] = p.dout("iw", [2, 4, 128, 32], F32)
    return o


def declare_proj_inputs(p):
    i = {}
    i["w_in"] = p.din("w_in", [D, DIN], F32)
    i["w_uk"] = p.din("w_uk", [10 * 512, 128], F32)
    i["gkv"] = p.din("gkv", [128, 512], F32)
    return i


def proj_setup(p, pi, acc, ACC):
    k = p.k
    p.wukT = p.sb("wukT", [128, 10, 512], BF16)
    p.WUK = Buf("wukT")
    p.gkv, p.GKV = load_cols(p, "gkv_s", pi["gkv"], 512)
    wuk_v = pi["w_uk"].rearrange("(a p) d -> p a d", p=128)
    for a0 in range(0, 40, 4):
        st, ST, sname = p.stagef()
        k.dma("sp", st[:].rearrange("p (a d) -> p a d", d=128), wuk_v[:, a0:a0 + 4, :], writes=[ST], sem=sname)
        ps, PS = p.ps_next()
        for a in range(4):
            k.op("pe", lambda e, a=a, ps=ps, st=st: e.transpose(out=ps[:, a * 128:(a + 1) * 128], in_=st[:, a * 128:(a + 1) * 128],
                                                                identity=p.identf[:]), reads=[ST, p.C], writes=[PS])
        h, cc0 = divmod(a0, 4)
        k.op("act", lambda e, ps=ps, h=h: e.activation(out=p.wukT[:, h, :], in_=ps[:], func=AF.Copy), reads=[PS], writes=[p.WUK])
    p.cqT = acc[:, 0:5, :].rearrange("p a t -> p (a t)").bitcast(BF16).rearrange("p (h t) -> p h t", t=512)
    p.CQ = ACC
    p.ckvr = acc[:, 8:12, :]
    p.CKV = ACC
    p.sm = p.sb("sm_small", [128, 16], F32)
    p.SM = Buf("sm")
    p.ikc = p.sb("ikc", [128, 128], F32)
    p.IKC = Buf("ikc")


def proj_chunk(p, pi, po, ch, xb, XB):
    k = p.k
    wv = pi["w_in"].rearrange("(kc p) n -> p kc n", p=128)

    def store_fm(dst, scale):
        def ev(ci, ps, PS):
            st, ST, sname = p.stage()
            k.op("act", lambda e: e.activation(out=st[:], in_=ps[:], func=AF.Copy, scale=scale), reads=[PS], writes=[ST])
            k.dma("sp", dst[ch, ci], st[:], reads=[ST], sem=sname)
        return ev

    dense_fm(p, wv, O_DQ, 1536, xb, XB, store_fm(po["qd"], SCALE))
    dense_fm(p, wv, O_DK, 1536, xb, XB, store_fm(po["kd"], 1.0))
    dense_fm(p, wv, O_MQ, 1280, xb, XB, store_fm(po["qm"], SCALE))
    dense_fm(p, wv, O_MK, 1280, xb, XB, store_fm(po["km"], 1.0))
    dense_fm(p, wv, O_IQ, 2048, xb, XB, store_fm(po["qi"], 0.125))

    def ev_cq(ci, ps, PS):
        k.op("act", lambda e: e.activation(out=p.cqT[:, ci, :], in_=ps[:], func=AF.Copy), reads=[PS], writes=[p.CQ])
    dense_fm(p, wv, O_CQ, 1280, xb, XB, ev_cq)
    for h in range(10):
        for cc in range(4):
            ps, PS = p.ps_next()
            k.pe_group([lambda e, h=h, cc=cc, ps=ps: e.matmul(ps[:], lhsT=p.wukT[:, h, cc * 128:(cc + 1) * 128], rhs=p.cqT[:, h, :],
                                                             start=True, stop=True)], reads=[p.WUK, p.CQ], writes=[PS])
            store_fm(po["ql"], SCALE)(h * 4 + cc, ps, PS)

    def store_tm(dst):
        def ev(tb, g0, w, ps, PS):
            st, ST, sname = p.stage()
            k.op("act", lambda e: e.activation(out=st[:, 0:w], in_=ps[:, 0:w], func=AF.Copy), reads=[PS], writes=[ST])
            k.dma("sp", dst[ch, tb, :, g0:g0 + w], st[:, 0:w], reads=[ST], sem=sname)
        return ev

    dense_tm(p, wv, O_DV, 1536, xb, XB, store_tm(po["vd"]))
    dense_tm(p, wv, O_MV, 1280, xb, XB, store_tm(po["vm"]))

    def ev_ckv(tb, g0, w, ps, PS):
        k.op("act", lambda e: e.activation(out=p.ckvr[:, tb, g0:g0 + w], in_=ps[:, 0:w], func=AF.Copy), reads=[PS], writes=[p.CKV])
    dense_tm(p, wv, O_CKV, 512, xb, XB, ev_ckv)
    sm = p.sm
    for tb in range(4):
        sq, SQ, _ = p.stagef()
        k.op("act", lambda e, tb=tb, sq=sq: e.activation(out=sq[:], in_=p.ckvr[:, tb, :], func=AF.Square), reads=[p.CKV], writes=[SQ])
        k.op("dve", lambda e, sq=sq: e.reduce_sum(out=sm[:, 0:1], in_=sq[:], axis=AX.X), reads=[SQ], writes=[p.SM])
        k.op("dve", lambda e: e.tensor_scalar(out=sm[:, 0:1], in0=sm[:, 0:1], scalar1=1.0 / 512, scalar2=1e-5, op0=ALU.mult, op1=ALU.add),
             reads=[p.SM], writes=[p.SM])
        k.op("act", lambda e: e.activation(out=sm[:, 0:1], in_=sm[:, 0:1], func=AF.Sqrt), reads=[p.SM], writes=[p.SM])
        k.op("dve", lambda e: e.reciprocal(out=sm[:, 0:1], in_=sm[:, 0:1]), reads=[p.SM], writes=[p.SM])
        k.op("dve", lambda e, tb=tb: e.scalar_tensor_tensor(out=p.ckvr[:, tb, :], in0=p.ckvr[:, tb, :], scalar=sm[:, 0:1], in1=p.gkv[:],
                                                            op0=ALU.mult, op1=ALU.mult), reads=[p.CKV, p.SM, p.GKV], writes=[p.CKV])
        st, ST, sname = p.stage()
        k.op("act", lambda e, tb=tb, st=st: e.activation(out=st[:], in_=p.ckvr[:, tb, :], func=AF.Copy), reads=[p.CKV], writes=[ST])
        k.dma("sp", po["kv"][ch, tb], st[:], reads=[ST], sem=sname)
        ps, PS = p.ps_next()
        for cc in range(4):
            k.op("pe", lambda e, cc=cc, tb=tb, ps=ps: e.transpose(out=ps[:, cc * 128:(cc + 1) * 128], in_=p.ckvr[:, tb, cc * 128:(cc + 1) * 128],
                                                                  identity=p.identf[:]), reads=[p.CKV, p.C], writes=[PS])
        st, ST, sname = p.stage()
        k.op("act", lambda e, st=st, ps=ps: e.activation(out=st[:], in_=ps[:], func=AF.Copy), reads=[PS], writes=[ST])
        k.dma("sp", po["kvT"][ch, :, :, tb * 128:(tb + 1) * 128].rearrange("c p t -> p c t"),
              st[:].rearrange("p (c t) -> p c t", t=128), reads=[ST], sem=sname)

    def ev_ik(tb, g0, w, ps, PS):
        ikc = p.ikc
        k.op("dve", lambda e: e.reduce_sum(out=sm[:, 1:2], in_=ps[:, 0:64], axis=AX.X), reads=[PS], writes=[p.SM])
        k.op("dve", lambda e: e.tensor_scalar(out=sm[:, 1:2], in0=sm[:, 1:2], scalar1=1.0 / 64, scalar2=None, op0=ALU.mult), reads=[p.SM], writes=[p.SM])
        k.op("dve", lambda e: e.tensor_scalar(out=ikc[:, 0:64], in0=ps[:, 0:64], scalar1=sm[:, 1:2], scalar2=None, op0=ALU.subtract),
             reads=[PS, p.SM], writes=[p.IKC])
        k.op("dve", lambda e: e.tensor_tensor(out=ikc[:, 64:128], in0=ikc[:, 0:64], in1=ikc[:, 0:64], op=ALU.mult), reads=[p.IKC], writes=[p.IKC])
        k.op("dve", lambda e: e.reduce_sum(out=sm[:, 2:3], in_=ikc[:, 64:128], axis=AX.X), reads=[p.IKC], writes=[p.SM])
        k.op("dve", lambda e: e.tensor_scalar(out=sm[:, 2:3], in0=sm[:, 2:3], scalar1=1.0 / 64, scalar2=LN_EPS, op0=ALU.mult, op1=ALU.add),
             reads=[p.SM], writes=[p.SM])
        k.op("act", lambda e: e.activation(out=sm[:, 2:3], in_=sm[:, 2:3], func=AF.Sqrt), reads=[p.SM], writes=[p.SM])
        k.op("dve", lambda e: e.reciprocal(out=sm[:, 2:3], in_=sm[:, 2:3]), reads=[p.SM], writes=[p.SM])
        k.op("dve", lambda e: e.tensor_scalar(out=ikc[:, 0:64], in0=ikc[:, 0:64], scalar1=sm[:, 2:3], scalar2=None, op0=ALU.mult),
             reads=[p.IKC, p.SM], writes=[p.IKC])
        st, ST, sname = p.stagef()
        k.op("dve", lambda e: e.tensor_scalar(out=st[:, 0:32], in0=ps[:, 64:96], scalar1=32 ** -0.5, scalar2=None, op0=ALU.mult),
             reads=[PS], writes=[ST])
        k.dma("sp", po["iw"][ch, tb], st[:, 0:32], reads=[ST], sem=sname)
        ps2, PS2 = p.ps_next()
        k.op("pe", lambda e: e.transpose(out=ps2[0:64, 0:128], in_=ikc[:, 0:64], identity=p.identf[:]), reads=[p.IKC, p.C], writes=[PS2])
        st2, ST2, sname2 = p.stage()
        k.op("act", lambda e: e.activation(out=st2[0:64, 0:128], in_=ps2[0:64, 0:128], func=AF.Copy), reads=[PS2], writes=[ST2])
        k.dma("sp", po["ikT"][ch, :, tb * 128:(tb + 1) * 128], st2[0:64, 0:128], reads=[ST2], sem=sname2)
    dense_tm(p, wv, O_IK, 96, xb, XB, ev_ik)


def build_A():
    p = Prog()
    k = p.k
    xT = p.din("xT", [2, 32, 128, 512], F32)
    g_d = p.din("ln_g", [128, 32], F32)
    b_d = p.din("ln_b", [128, 32], F32)
    ident_d = p.din("ident", [128, 128], F32)
    ones_d = p.din("ones", [128, 128], F32)
    pi = declare_proj_inputs(p)
    po = declare_proj_outputs(p)
    res = p.dout("res", [2, 32, 128, 512], F32)
    p.mk_psum()
    p.mk_consts(ident_d, ones_d)
    p.mk_stage()
    p.mk_wslots(4)
    acc = p.sb("acc", [128, 32, 512], F32)
    ACC = Buf("acc")
    xb = p.sb("xb", [128, 32, 512], BF16)
    XB = Buf("xb")
    p.ln_mu = p.sb("ln_mu", [128, 512], F32)
    p.ln_rstd = p.sb("ln_rstd", [128, 512], F32)
    p.LNS = Buf("lns")
    gcol, G1 = load_cols(p, "gcol", g_d, 32)
    bcol, G2 = load_cols(p, "bcol", b_d, 32)
    GB = Buf("gb")
    k.op("dve", lambda e: e.tensor_copy(out=gcol[:], in_=gcol[:]), reads=[G1, G2], writes=[GB])
    proj_setup(p, pi, acc, ACC)
    for ch in range(2):
        for q in range(4):
            k.dma("sp", acc[:, q * 8:(q + 1) * 8, :], xT[ch, q * 8:(q + 1) * 8].rearrange("a p t -> p a t"), writes=[ACC], sem="ld%d" % q)
        layernorm_fm(p, acc, ACC, xb, XB, gcol, bcol, GB)
        for q in range(4):
            k.dma("sp", res[ch, q * 8:(q + 1) * 8].rearrange("a p t -> p a t"), acc[:, q * 8:(q + 1) * 8, :], reads=[ACC], sem="sr%d" % q)
        proj_chunk(p, pi, po, ch, xb, XB)
    k.finish()
    return p


def build_C(with_proj):
    p = Prog()
    k = p.k
    yT = p.din("yT", [2, 32, 128, 512], BF16)
    res_in = p.din("res_in", [2, 32, 128, 512], F32)
    w_o = p.din("w_o", [D, D], F32)
    w_up = p.din("w_up", [D, DFF], F32)
    w_down = p.din("w_down", [DFF, D], F32)
    g1_d = p.din("ln1_g", [128, 32], F32)
    b1_d = p.din("ln1_b", [128, 32], F32)
    g2_d = p.din("ln2_g", [128, 32], F32)
    b2_d = p.din("ln2_b", [128, 32], F32)
    ident_d = p.din("ident", [128, 128], F32)
    ones_d = p.din("ones", [128, 128], F32)
    if with_proj:
        pi = declare_proj_inputs(p)
        po = declare_proj_outputs(p)
    res = p.dout("res", [2, 32, 128, 512], F32)
    p.mk_psum()
    p.mk_consts(ident_d, ones_d)
    p.mk_stage()
    p.mk_wslots(4)
    acc = p.sb("acc", [128, 32, 512], F32)
    ACC = Buf("acc")
    xb = p.sb("xb", [128, 32, 512], BF16)
    XB = Buf("xb")
    hT = [(p.sb("hT%d" % i, [128, 512], BF16), Buf("hT%d" % i)) for i in range(4)]
    p.ln_mu = p.sb("ln_mu", [128, 512], F32)
    p.ln_rstd = p.sb("ln_rstd", [128, 512], F32)
    p.LNS = Buf("lns")
    g1, A1 = load_cols(p, "g1", g1_d, 32)
    b1, A2 = load_cols(p, "b1", b1_d, 32)
    g2, A3 = load_cols(p, "g2", g2_d, 32)
    b2, A4 = load_cols(p, "b2", b2_d, 32)
    GB = Buf("gb")
    k.op("dve", lambda e: e.tensor_copy(out=g1[:], in_=g1[:]), reads=[A1, A2, A3, A4], writes=[GB])
    if with_proj:
        proj_setup(p, pi, acc, ACC)
    wo_v = w_o.rearrange("(kc p) n -> p kc n", p=128)
    wu_v = w_up.rearrange("(kc p) n -> p kc n", p=128)
    wd_v = w_down.rearrange("(a p) n -> p a n", p=128)
    accf = acc[:].rearrange("p a t -> p (a t)")
    for ch in range(2):
        for q in range(4):
            k.dma("sp", acc[:, q * 8:(q + 1) * 8, :], res_in[ch, q * 8:(q + 1) * 8].rearrange("a p t -> p a t"), writes=[ACC], sem="ld%d" % q)
            k.dma("sp", xb[:, q * 8:(q + 1) * 8, :], yT[ch, q * 8:(q + 1) * 8].rearrange("a p t -> p a t"), writes=[XB], sem="ly%d" % q)
        k.op("act", lambda e: e.activation(out=accf, in_=accf, func=AF.Copy, scale=ALPHA), reads=[ACC], writes=[ACC])

        def ev_add(ci, ps, PS):
            k.op("dve", lambda e: e.tensor_tensor(out=acc[:, ci, :], in0=ps[:], in1=acc[:, ci, :], op=ALU.add), reads=[PS, ACC], writes=[ACC])
        dense_fm(p, wo_v, 0, D, xb, XB, ev_add)
        layernorm_fm(p, acc, ACC, xb, XB, g1, b1, GB)
        k.op("act", lambda e: e.activation(out=accf, in_=accf, func=AF.Copy, scale=ALPHA), reads=[ACC], writes=[ACC])
        NG = DFF // 256
        state = {}

        def up(g):
            slot, SL, sname = p.w_next()
            sv = slot[:].rearrange("p (a b) -> p a b", b=256)
            k.dma("pool", sv, wu_v[:, :, g * 256:(g + 1) * 256], writes=[SL], sem=sname)
            hs = []
            for c in range(2):
                ps, PS = p.ps_next(0, 6)
                k.pe_group([(lambda e, kc=kc, c=c, ps=ps, sv=sv: e.matmul(ps[:], lhsT=sv[:, kc, c * 128:(c + 1) * 128], rhs=xb[:, kc, :],
                                                                          start=(kc == 0), stop=(kc == 31))) for kc in range(32)],
                           reads=[SL, XB], writes=[PS])
                rt, RT, _ = p.stagef()
                k.op("act", lambda e, rt=rt, ps=ps: e.activation(out=rt[:], in_=ps[:], func=AF.Relu), reads=[PS], writes=[RT])
                h, H = hT[(g % 2) * 2 + c]
                k.op("dve", lambda e, rt=rt, h=h: e.tensor_tensor(out=h[:], in0=rt[:], in1=rt[:], op=ALU.mult), reads=[RT], writes=[H])
                hs.append((h, H))
            state[g] = hs

        def down(g):
            slot, SL, sname = p.w_next()
            sv = slot[:].rearrange("p (a b) -> p a b", b=4096)
            k.dma("pool", sv, wd_v[:, g * 2:(g + 1) * 2, :], writes=[SL], sem=sname)
            hs = state.pop(g)
            for dc in range(32):
                ps, PS = p.ps_next(0, 6)
                k.pe_group([(lambda e, c=c, dc=dc, ps=ps, sv=sv: e.matmul(ps[:], lhsT=sv[:, c, dc * 128:(dc + 1) * 128], rhs=hs[c][0][:],
                                                                          start=(c == 0), stop=(c == 1))) for c in range(2)],
                           reads=[SL, hs[0][1], hs[1][1]], writes=[PS])
                ev_add(dc, ps, PS)

        up(0)
        for g in range(NG):
            if g + 1 < NG:
                up(g + 1)
            down(g)
        layernorm_fm(p, acc, ACC, xb, XB, g2, b2, GB)
        for q in range(4):
            k.dma("sp", res[ch, q * 8:(q + 1) * 8].rearrange("a p t -> p a t"), acc[:, q * 8:(q + 1) * 8, :], reads=[ACC], sem="sr%d" % q)
        if with_proj:
            proj_chunk(p, pi, po, ch, xb, XB)
    k.finish()
    return p


_CACHE = {}


def _prog(name, fn, *a):
    key = (name,) + a
    if key not in _CACHE:
        _CACHE[key] = fn(*a)
    return _CACHE[key]


def _chunks(c):
    r = c % 4
    return c // 4, (r, 7 - r)


def _fm(a2d):
    return np.ascontiguousarray(a2d.T).reshape(32, 128, 512)


def _cols(v):
    return np.ascontiguousarray(v.reshape(-1, 128).T)


def _consts():
    return {"ident": np.eye(128, dtype=np.float32), "ones": np.ones((128, 128), np.float32)}


def _proj_inputs(l, w_in, w_uk, kv_norm_g):
    return {"w_in": w_in[l], "w_uk": np.ascontiguousarray(w_uk[l].reshape(10 * 512, 128)),
            "gkv": np.ascontiguousarray(np.broadcast_to(kv_norm_g[l][None, :], (128, 512)))}


def _run(p, in_maps):
    res = run_bass_kernel_spmd(p.nc, in_maps, core_ids=list(range(8)))
    return res.results


def run_A(x, ln_emb_g, ln_emb_b, w_in, w_uk, kv_norm_g):
    p = _prog("A", build_A)
    maps = []
    for c in range(8):
        b, qs = _chunks(c)
        m = {"xT": np.stack([_fm(x[b, q * 512:(q + 1) * 512, :]) for q in qs]),
             "ln_g": _cols(ln_emb_g), "ln_b": _cols(ln_emb_b)}
        m.update(_consts())
        m.update(_proj_inputs(0, w_in, w_uk, kv_norm_g))
        maps.append(m)
    return _run(p, maps)


UCH = (16, 32)


def build_B(layer_idx):
    lam_init = 0.8 - 0.6 * math.exp(-0.3 * layer_idx)
    p = Prog()
    k = p.k
    qd = p.din("qd", [2, 12, 128, 512], BF16)
    qm = p.din("qm", [2, 10, 128, 512], BF16)
    ql = p.din("ql", [2, 40, 128, 512], BF16)
    qi = p.din("qi", [2, 16, 128, 512], BF16)
    iw = p.din("iw", [2, 4, 128, 32], F32)
    KD = p.din("KD", [2, 12, 128, 4096], BF16)
    VD = p.din("VD", [2, 32, 128, 1536], BF16)
    KM = p.din("KM", [2, 10, 128, 4096], BF16)
    VM = p.din("VM", [2, 32, 128, 1280], BF16)
    KVT = p.din("KVT", [2, 4, 128, 4096], BF16)
    KV = p.din("KV", [2, 32, 128, 512], BF16)
    IKT = p.din("IKT", [2, 128, 4096], BF16)
    GBd = p.din("GB", [32, 128, 1024], F32)
    c31d = p.din("c31", [128, 32], F32)
    tmd = p.din("tm", [128, 64], F32)
    tmId = p.din("tmI", [128, 64], F32)
    tmBd = p.din("tmB", [128, 32], F32)
    lamd = p.din("lamv", [128, 512], F32)
    gsd = p.din("gsub", [128, 2], F32)
    wuvd = p.din("w_uv", [10 * 512, 128], F32)
    Ed = p.din("Esel", [16, 2048], F32)
    CMd = p.din("CM", [128, 128], F32)
    ident_d = p.din("ident", [128, 128], F32)
    ones_d = p.din("ones", [128, 128], F32)
    yT = p.dout("yT", [2, 32, 128, 512], BF16)
    p.mk_psum()
    p.mk_consts(ident_d, ones_d)
    p.mk_stage(4)
    c31, C31 = load_cols(p, "c31s", c31d, 32)
    tm, TM = load_cols(p, "tms", tmd, 64)
    tmI, TMI = load_cols(p, "tmIs", tmId, 64)
    tmB, TMB = load_cols(p, "tmBs", tmBd, 32)
    CM, CMB = load_cols(p, "CMs", CMd, 128)
    gs, GS = load_cols(p, "gss", gsd, 2)
    KA = p.sb("KA", [128, 16384], BF16)
    VA = p.sb("VA", [128, 16384], BF16)
    KAb = [Buf("KA0"), Buf("KA1")]
    VAb = [Buf("VA0"), Buf("VA1")]
    nmT = p.sb("nmT", [128, 32, 512], BF16)
    NMT = Buf("nmT")
    sc = p.sb("sc", [128, 4096], F32)
    SC = Buf("sc")
    wk = p.sb("wk", [128, 4096], F32)
    WK = Buf("wk")
    m8 = p.sb("m8", [128, 8], F32)
    M8 = Buf("m8")
    iqs = [(p.sb("iq%d" % i, [128, 16, 128], BF16), Buf("iq%d" % i)) for i in range(2)]
    ikt = p.sb("ikt", [128, 4096], BF16)
    IKB = Buf("ikt")
    pts = [(p.sb("pt%d" % i, [128, 512], BF16), Buf("pt%d" % i)) for i in range(3)]
    rts = p.stgf
    SCg = [Buf("sc%d" % i) for i in range(8)]
    qhs = [(p.sb("qh%d" % i, [128, 2048], BF16), Buf("qh%d" % i)) for i in range(2)]
    ghs = [(p.sb("gh%d" % i, [128, 1024], BF16), Buf("gh%d" % i)) for i in range(2)]
    om = p.sb("om", [128, 4, 512], F32)
    OM = Buf("om")
    olb = p.sb("olb", [128, 4, 512], BF16)
    OLB = Buf("olb")
    rden = p.sb("rden", [128, 512], F32)
    RD = Buf("rden")
    wuv = p.sb("wuv", [128, 40, 128], BF16)
    WUV = Buf("wuv")
    Es = p.sb("Es", [16, 2048], BF16)
    ESB = Buf("Es")
    fb = p.sb("fb", [128, 64], F32)
    FB = Buf("fb")
    sm = p.sb("smB", [128, 64], F32)
    SM = Buf("smB")
    wsg = p.sb("wsg", [128, 64], F32)
    WSG = Buf("wsg")
    gt = p.sb("gt", [128, 16], F32)
    GT = Buf("gt")
    snT = p.sb("snT", [16, 512], BF16)
    SNT = Buf("snT")
    kmT = p.sb("kmT", [128, 16], F32)
    kmTb = p.sb("kmTb", [128, 16], BF16)
    KMT = Buf("kmT")
    k.dma("pool", wuv[:], wuvd.rearrange("(a p) d -> p a d", p=128), writes=[WUV], sem="cw")
    k.dma("pool", Es[:], Ed[:, :], writes=[ESB], sem="cw")
    lamt, LAMT = wk, WK
    k.dma("sp", wk[:, 0:512], lamd[:, :], writes=[WK], sem="c0")
    k.op("dve", lambda e: e.tensor_tensor(out=sc[:, 0:128], in0=lamt[:, 0:128], in1=lamt[:, 128:256], op=ALU.mult), reads=[LAMT], writes=[SC])
    k.op("dve", lambda e: e.tensor_tensor(out=sc[:, 128:256], in0=lamt[:, 256:384], in1=lamt[:, 384:512], op=ALU.mult), reads=[LAMT], writes=[SC])
    k.op("dve", lambda e: e.reduce_sum(out=sm[:, 1:2], in_=sc[:, 0:128], axis=AX.X), reads=[SC], writes=[SM])
    k.op("dve", lambda e: e.reduce_sum(out=sm[:, 2:3], in_=sc[:, 128:256], axis=AX.X), reads=[SC], writes=[SM])
    k.op("act", lambda e: e.activation(out=sm[:, 1:3], in_=sm[:, 1:3], func=AF.Exp), reads=[SM], writes=[SM])
    k.op("dve", lambda e: e.scalar_tensor_tensor(out=sm[:, 0:1], in0=sm[:, 2:3], scalar=-lam_init, in1=sm[:, 1:2], op0=ALU.add, op1=ALU.subtract),
         reads=[SM], writes=[SM])
    k.op("dve", lambda e: e.tensor_scalar(out=gs[:], in0=gs[:], scalar1=1.0 - lam_init, scalar2=None, op0=ALU.mult), reads=[GS], writes=[GS])

    st = {"s": 0, "pt": 0, "q": 0, "g": 0, "ka": 0, "va": 0, "rt": 0}

    def rot(lst, key):
        i = st[key] % len(lst)
        st[key] += 1
        return lst[i] + (i,)

    def load_bias(col, ch):
        gh, GH, gi = rot(ghs, "g")
        k.dma("pool", gh[:], GBd[col], writes=[GH], sem="g%d" % gi)
        k.op("dve", lambda e: e.tensor_scalar(out=fb[:, 0:32], in0=tm[:, ch * 32:(ch + 1) * 32], scalar1=c31[:, col:col + 1], scalar2=None, op0=ALU.add),
             reads=[TM, C31], writes=[FB])
        return gh, GH

    def attn_loop(ch, U, s_parts, s_reads, gh, GH, av_parts, av_reads, nacc):
        accB = [p.ps[i][1] for i in range(nacc)] + [p.ps[4][1]]
        for u in range(U):
            S, SB = p.ps[5 + (st["s"] % 2)]
            st["s"] += 1
            parts = list(s_parts(u))
            rd = list(s_reads) + [p.C]
            if u <= 4:
                parts.append((p.identb[:], gh[:, 128 * u:128 * u + 512]))
                rd.append(GH)
            n = len(parts)
            k.pe_group([(lambda e, i=i, S=S, pr=pr: e.matmul(S[:], lhsT=pr[0], rhs=pr[1], start=(i == 0), stop=(i == n - 1)))
                        for i, pr in enumerate(parts)], reads=rd, writes=[SB])
            pt, PT, _ = rot(pts, "pt")
            if u <= 3:
                k.op("act", lambda e, S=S, pt=pt: e.activation(out=pt[:], in_=S[:], func=AF.Exp), reads=[SB], writes=[PT])
            elif u == 4:
                k.op("act", lambda e, S=S, pt=pt: e.activation(out=pt[:], in_=S[:], func=AF.Exp, bias=tm[:, ch * 32 + 4:ch * 32 + 5]),
                     reads=[SB, TM], writes=[PT])
            else:
                k.op("act", lambda e, S=S, pt=pt, u=u: e.activation(out=pt[:], in_=S[:], func=AF.Exp, bias=fb[:, u:u + 1]),
                     reads=[SB, FB], writes=[PT])
            av = list(av_parts(u)) + [(4, p.onesb[:])]
            k.pe_group([(lambda e, b=b, l=l, pt=pt: e.matmul(p.ps[b][0][:], lhsT=l, rhs=pt[:], start=(u == 0), stop=(u == U - 1)))
                        for b, l in av], reads=[PT, p.C] + list(av_reads), writes=accB)
        k.op("dve", lambda e: e.reciprocal(out=rden[:], in_=p.ps[4][0][:]), reads=[p.ps[4][1]], writes=[RD])

    def store_y(ch, idx, src_fn, reads):
        stg, ST, sname = p.stage()
        k.op("dve", lambda e: src_fn(e, stg), reads=reads, writes=[ST])
        k.dma("sp", yT[ch, idx], stg[:], reads=[ST], sem=sname)

    for ch in range(2):
        U = UCH[ch]
        NK = U * 128
        kvT = KA[:].rearrange("p (c s) -> p c s", s=4096)
        kvv = VA[:].rearrange("p (u c) -> p u c", c=512)
        k.dma("pool", kvT[:, :, 0:NK], KVT[ch, :, :, 0:NK].rearrange("c p s -> p c s"), writes=KAb, sem="ka")
        k.dma("pool", kvv[:, 0:U, :], KV[ch, 0:U].rearrange("u p c -> p u c"), writes=VAb, sem="va")
        k.dma("pool", ikt[:, 0:NK], IKT[ch, :, 0:NK], writes=[IKB], sem="ik")
        for qb in range(4):
            iq, IQ, ii = rot(iqs, "q")
            k.dma("pool", iq[:], qi[ch, :, :, qb * 128:(qb + 1) * 128].rearrange("a p t -> p a t"), writes=[IQ], sem="iq%d" % ii)
            wst, WST, wname = wsg, WSG, "wsg"
            k.dma("sp", wst[:, 0:32], iw[ch, qb], writes=[WST], sem=wname)
            k.op("act", lambda e, wst=wst: e.activation(out=sm[:, 8:40], in_=wst[:, 0:32], func=AF.Abs), reads=[WST], writes=[SM])
            k.op("act", lambda e, wst=wst: e.activation(out=wst[:, 32:64], in_=wst[:, 0:32], func=AF.Sign), reads=[WST], writes=[WST])
            for g in range(U // 4):
                for h in range(32):
                    pb = 64 * (h % 2)
                    ps, PS = p.ps_next(0, 4)
                    k.pe_group([lambda e, ps=ps, pb=pb, h=h, g=g, iq=iq: e.matmul(ps[:], lhsT=iq[pb:pb + 64, h // 2, :], rhs=ikt[pb:pb + 64, g * 512:(g + 1) * 512],
                                                                                  start=True, stop=True)], reads=[IQ, IKB], writes=[PS])
                    rt, RT, _ = rot(rts, "rt")
                    k.op("act", lambda e, rt=rt, ps=ps, h=h: e.activation(out=rt[:], in_=ps[:], func=AF.Relu, scale=sm[:, 8 + h:9 + h]),
                         reads=[PS, SM], writes=[RT])
                    eng = "dve"
                    if h == 0:
                        k.op(eng, lambda e, rt=rt, g=g, wst=wst: e.tensor_scalar(out=sc[:, g * 512:(g + 1) * 512], in0=rt[:], scalar1=wst[:, 32:33], scalar2=None,
                                                                                  op0=ALU.mult), reads=[RT, WST], writes=[SCg[g]])
                    else:
                        k.op(eng, lambda e, rt=rt, g=g, h=h, wst=wst: e.scalar_tensor_tensor(out=sc[:, g * 512:(g + 1) * 512], in0=rt[:], scalar=wst[:, 32 + h:33 + h],
                                                                                              in1=sc[:, g * 512:(g + 1) * 512], op0=ALU.mult, op1=ALU.add),
                             reads=[RT, WST, SCg[g]], writes=[SCg[g]])
            allsc = SCg[0:U // 4]
            if qb < 3:
                k.op("dve", lambda e, qb=qb: e.memset(sc[:, 0:(3 - qb) * 128], -1e30), writes=allsc)
            k.op("dve", lambda e, qb=qb: e.tensor_tensor(out=sc[:, (3 - qb) * 128:(4 - qb) * 128], in0=sc[:, (3 - qb) * 128:(4 - qb) * 128], in1=CM[:], op=ALU.add),
                 reads=allsc + [CMB], writes=allsc)
            for u in range(4, U):
                k.op("dve", lambda e, u=u: e.tensor_scalar(out=sc[:, u * 128:(u + 1) * 128], in0=sc[:, u * 128:(u + 1) * 128], scalar1=tmI[:, ch * 32 + u:ch * 32 + u + 1],
                                                           scalar2=None, op0=ALU.add), reads=allsc + [TMI], writes=allsc)
            k.op("pool", lambda e: e.tensor_copy(out=wk[:, 0:NK], in_=sc[:, 0:NK]), reads=allsc, writes=[WK])
            for r in range(32):
                k.op("dve", lambda e: e.max(out=m8[:], in_=wk[:, 0:NK]), reads=[WK], writes=[M8])
                if r < 31:
                    k.op("dve", lambda e: e.match_replace(out=wk[:, 0:NK], in_to_replace=m8[:], in_values=wk[:, 0:NK], imm_value=-1e30), reads=[WK, M8], writes=[WK])
            k.op("dve", lambda e: e.tensor_scalar(out=m8[:, 7:8], in0=m8[:, 7:8], scalar1=-1e29, scalar2=None, op0=ALU.max), reads=[M8], writes=[M8])
            k.op("dve", lambda e: e.tensor_scalar(out=wk[:, 0:NK], in0=sc[:, 0:NK], scalar1=m8[:, 7:8], scalar2=NEGM, op0=ALU.is_lt, op1=ALU.mult),
                 reads=allsc + [M8], writes=[WK])
            for g in range(U // 4):
                ps, PS = p.ps_next(0, 4)
                for a in range(4):
                    u = g * 4 + a
                    k.op("pe", lambda e, ps=ps, a=a, u=u: e.transpose(out=ps[:, a * 128:(a + 1) * 128], in_=wk[:, u * 128:(u + 1) * 128], identity=p.identf[:]),
                         reads=[WK, p.C], writes=[PS])
                k.op("act", lambda e, ps=ps, g=g, qb=qb: e.activation(out=nmT[:, g * 4:(g + 1) * 4, qb * 128:(qb + 1) * 128],
                                                                      in_=ps[:].rearrange("p (a t) -> p a t", t=128), func=AF.Copy), reads=[PS], writes=[NMT])
        for h in range(10):
            gh, GH = load_bias(22 + h, ch)
            qh, QH, qhi = rot(qhs, "q")
            qv = qh[:].rearrange("p (c t) -> p c t", t=512)
            k.dma("pool", qv, ql[ch, h * 4:(h + 1) * 4].rearrange("c p t -> p c t"), writes=[QH], sem="qh%d" % qhi)

            def s_parts(u, qv=qv):
                return [(kvT[:, cc, u * 128:(u + 1) * 128], qv[:, cc, :]) for cc in range(4)] + [(p.identb[:], nmT[:, u, :])]

            def av_parts(u):
                return [(cc, kvv[:, u, cc * 128:(cc + 1) * 128]) for cc in range(4)]
            attn_loop(ch, U, s_parts, KAb + [QH, NMT], gh, GH, av_parts, VAb, 4)
            for cc in range(4):
                k.op("act", lambda e, cc=cc: e.activation(out=olb[:, cc, :], in_=p.ps[cc][0][:], func=AF.Copy), reads=[p.ps[cc][1]], writes=[OLB])
            ps, PS = p.ps[7]
            k.pe_group([(lambda e, cc=cc, h=h: e.matmul(ps[:], lhsT=wuv[:, h * 4 + cc, :], rhs=olb[:, cc, :], start=(cc == 0), stop=(cc == 3)))
                        for cc in range(4)], reads=[WUV, OLB], writes=[PS])
            store_y(ch, 22 + h, lambda e, stg, ps=ps: e.tensor_tensor(out=stg[:], in0=ps[:], in1=rden[:], op=ALU.mult), [PS, RD])

        for h in range(6):
            kslot = st["ka"] % 2
            st["ka"] += 1
            kd = KA[:, kslot * 8192:(kslot + 1) * 8192].rearrange("p (m s) -> p m s", s=4096)
            vd = VA[:, kslot * 8192:(kslot + 1) * 8192].rearrange("p (u c) -> p u c", c=256)
            k.dma("pool", kd[:, :, 0:NK], KD[ch, 2 * h:2 * h + 2, :, 0:NK].rearrange("m p s -> p m s"), writes=[KAb[kslot]], sem="ka%d" % kslot)
            k.dma("pool", vd[:, 0:U, :], VD[ch, 0:U, :, h * 256:(h + 1) * 256].rearrange("u p c -> p u c"), writes=[VAb[kslot]], sem="va%d" % kslot)
            qh, QH, qhi = rot(qhs, "q")
            qv = qh[:].rearrange("p (c t) -> p c t", t=512)
            k.dma("pool", qv[:, 0:2, :], qd[ch, 2 * h:2 * h + 2].rearrange("c p t -> p c t"), writes=[QH], sem="qh%d" % qhi)
            for m in range(2):
                gh, GH = load_bias(2 * h + m, ch)

                def s_parts(u, m=m, kd=kd, qv=qv):
                    return [(kd[:, m, u * 128:(u + 1) * 128], qv[:, m, :])]

                def av_parts(u, vd=vd):
                    return [(cc, vd[:, u, cc * 128:(cc + 1) * 128]) for cc in range(2)]
                attn_loop(ch, U, s_parts, [KAb[kslot], QH], gh, GH, av_parts, [VAb[kslot]], 2)
                for cc in range(2):
                    k.op("dve", lambda e, cc=cc, m=m: e.tensor_tensor(out=om[:, m * 2 + cc, :], in0=p.ps[cc][0][:], in1=rden[:], op=ALU.mult),
                         reads=[p.ps[cc][1], RD], writes=[OM])
            ps, PS = p.ps[7]
            for cc in range(2):
                k.op("dve", lambda e, cc=cc: e.scalar_tensor_tensor(out=om[:, cc, :], in0=om[:, 2 + cc, :], scalar=sm[:, 0:1], in1=om[:, cc, :],
                                                                    op0=ALU.mult, op1=ALU.add), reads=[OM, SM], writes=[OM])
                sq, SQ, _ = p.stagef()
                k.op("act", lambda e, cc=cc, sq=sq: e.activation(out=sq[:], in_=om[:, cc, :], func=AF.Square), reads=[OM], writes=[SQ])
                k.pe_group([lambda e, cc=cc, sq=sq: e.matmul(ps[:], lhsT=p.onesf[:], rhs=sq[:], start=(cc == 0), stop=(cc == 1))], reads=[SQ, p.C], writes=[PS])
            k.op("dve", lambda e: e.tensor_scalar(out=rden[:], in0=ps[:], scalar1=1.0 / 256, scalar2=1e-5, op0=ALU.mult, op1=ALU.add), reads=[PS], writes=[RD])
            k.op("act", lambda e: e.activation(out=rden[:], in_=rden[:], func=AF.Sqrt), reads=[RD], writes=[RD])
            k.op("dve", lambda e: e.reciprocal(out=rden[:], in_=rden[:]), reads=[RD], writes=[RD])
            for cc in range(2):
                store_y(ch, 2 * h + cc, lambda e, stg, cc=cc: e.scalar_tensor_tensor(out=stg[:], in0=om[:, cc, :], scalar=gs[:, cc:cc + 1], in1=rden[:],
                                                                                      op0=ALU.mult, op1=ALU.mult), [OM, RD, GS])

        NBK = U // 2
        for h in range(10):
            kslot = st["ka"] % 2
            st["ka"] += 1
            km = KA[:, kslot * 8192:kslot * 8192 + 4096]
            vm = VA[:, kslot * 8192:kslot * 8192 + 4096].rearrange("p (u c) -> p u c", c=128)
            k.dma("pool", km[:, 0:NK], KM[ch, h, :, 0:NK], writes=[KAb[kslot]], sem="ka%d" % kslot)
            k.dma("pool", vm[:, 0:U, :], VM[ch, 0:U, :, h * 128:(h + 1) * 128].rearrange("u p c -> p u c"), writes=[VAb[kslot]], sem="va%d" % kslot)
            qh, QH, qhi = rot(qhs, "q")
            k.dma("pool", qh[:, 0:512], qm[ch, h], writes=[QH], sem="qh%d" % qhi)
            gh, GH = load_bias(12 + h, ch)
            k.op("dve", lambda e, km=km: e.tensor_reduce(out=kmT[:, 0:NBK], in_=km[:, 0:NK].rearrange("p (v s) -> p v s", s=256), axis=AX.X, op=ALU.add),
                 reads=[KAb[kslot]], writes=[KMT])
            k.op("dve", lambda e: e.tensor_copy(out=kmTb[:, 0:NBK], in_=kmT[:, 0:NBK]), reads=[KMT], writes=[KMT])
            for qb in range(4):
                vown = 1 if qb < 2 else 0
                ps, PS = p.ps_next(0, 4)
                k.pe_group([lambda e, ps=ps, qb=qb, qh=qh: e.matmul(ps[:, 0:NBK], lhsT=qh[:, qb * 128:(qb + 1) * 128], rhs=kmTb[:, 0:NBK], start=True, stop=True)],
                           reads=[QH, KMT], writes=[PS])
                k.op("dve", lambda e, ps=ps: e.tensor_tensor(out=gt[:, 0:NBK], in0=ps[:, 0:NBK], in1=tmB[:, ch * 16:ch * 16 + NBK], op=ALU.add),
                     reads=[PS, TMB], writes=[GT])
                if NBK < 16:
                    k.op("dve", lambda e: e.memset(gt[:, NBK:16], -1e30), writes=[GT])
                k.op("dve", lambda e, vown=vown: e.memset(gt[:, 0:vown + 1], -1e30), writes=[GT])
                k.op("dve", lambda e: e.max(out=m8[:], in_=gt[:]), reads=[GT], writes=[M8])
                k.op("dve", lambda e: e.tensor_scalar(out=m8[:, 2:3], in0=m8[:, 2:3], scalar1=-1e29, scalar2=None, op0=ALU.max), reads=[M8], writes=[M8])
                k.op("dve", lambda e: e.tensor_scalar(out=gt[:], in0=gt[:], scalar1=m8[:, 2:3], scalar2=NEGM, op0=ALU.is_lt, op1=ALU.mult), reads=[GT, M8], writes=[GT])
                k.op("dve", lambda e, vown=vown: e.memset(gt[:, vown:vown + 1], 0.0), writes=[GT])
                ps2, PS2 = p.ps_next(0, 4)
                k.op("pe", lambda e, ps2=ps2: e.transpose(out=ps2[0:16, 0:128], in_=gt[:], identity=p.identf[:]), reads=[GT, p.C], writes=[PS2])
                k.op("act", lambda e, ps2=ps2, qb=qb: e.activation(out=snT[:, qb * 128:(qb + 1) * 128], in_=ps2[0:16, 0:128], func=AF.Copy), reads=[PS2], writes=[SNT])

            def s_parts(u, km=km, qh=qh):
                return [(km[:, u * 128:(u + 1) * 128], qh[:, 0:512]), (Es[:, (u // 2) * 128:(u // 2 + 1) * 128], snT[:, :])]

            def av_parts(u, vm=vm):
                return [(0, vm[:, u, :])]
            attn_loop(ch, U, s_parts, [KAb[kslot], QH, SNT, ESB], gh, GH, av_parts, [VAb[kslot]], 1)
            store_y(ch, 12 + h, lambda e, stg: e.tensor_tensor(out=stg[:], in0=p.ps[0][0][:], in1=rden[:], op=ALU.mult), [p.ps[0][1], RD])
    k.finish()
    return p


def _t5_bucket_np(d):
    n = np.maximum(d, 0)
    nf = np.maximum(n, 1).astype(np.float32)
    large = 16 + (np.log(nf / np.float32(16)) / np.float32(math.log(8.0)) * np.float32(16)).astype(np.int32)
    large = np.minimum(large, 31)
    return np.where(n < 16, n, large)


def _bias_layout(rel_bias):
    s = np.arange(128)[:, None]
    w = np.arange(1024)[None, :]
    d = w - 384 - s
    idx = _t5_bucket_np(d)
    g = rel_bias[idx]
    g = np.where((d >= 0)[:, :, None], g, np.float32(NEGM))
    return np.ascontiguousarray(g.transpose(2, 0, 1)).astype(np.float32)


def _gather_full(outs, name, b, axis_tok, tok_per_chunk):
    parts = []
    for Q in range(8):
        r = Q if Q < 4 else 7 - Q
        li = 0 if Q < 4 else 1
        parts.append(np.asarray(outs[b * 4 + r][name][li]))
    return np.concatenate(parts, axis=axis_tok)


def _rel_tiles(full, Q, axis, tile):
    out = np.zeros_like(np.take(full, np.arange(32 * tile), axis=axis))
    for u in range(32):
        j = 4 * Q + 3 - u
        if j < 0:
            break
        src = [slice(None)] * full.ndim
        dst = [slice(None)] * full.ndim
        src[axis] = slice(j * tile, (j + 1) * tile)
        dst[axis] = slice(u * tile, (u + 1) * tile)
        out[tuple(dst)] = full[tuple(src)]
    return out


def run_B(l, outs, rel_bias, diff_lambda, diff_subln_g, w_uv):
    p = _prog("B", build_B, l)
    GB = _bias_layout(rel_bias)
    c31 = np.ascontiguousarray(np.broadcast_to(rel_bias[31][None, :], (128, 32))).astype(np.float32)
    lamv = np.ascontiguousarray(np.broadcast_to(diff_lambda[l].reshape(1, 512), (128, 512))).astype(np.float32)
    gsub = _cols(diff_subln_g[l])
    Esel = np.zeros((16, 2048), np.float32)
    for n in range(16):
        Esel[n, n * 128:(n + 1) * 128] = 1.0
    CM = np.where(np.arange(128)[None, :] > np.arange(128)[:, None], np.float32(-1e30), np.float32(0)).astype(np.float32)
    full = {}
    for b in range(2):
        full[b] = {
            "KD": _gather_full(outs, "kd", b, 2, 512),
            "VD": _gather_full(outs, "vd", b, 0, 4),
            "KM": _gather_full(outs, "km", b, 2, 512),
            "VM": _gather_full(outs, "vm", b, 0, 4),
            "KVT": _gather_full(outs, "kvT", b, 2, 512),
            "KV": _gather_full(outs, "kv", b, 0, 4),
            "IKT": _gather_full(outs, "ikT", b, 1, 512),
        }
    maps = []
    for c in range(8):
        b, qs = _chunks(c)
        f = full[b]
        m = {n: np.asarray(outs[c][n]) for n in ("qd", "qm", "ql", "qi", "iw")}
        m["KD"] = np.stack([_rel_tiles(f["KD"], Q, 2, 128) for Q in qs])
        m["KM"] = np.stack([_rel_tiles(f["KM"], Q, 2, 128) for Q in qs])
        m["KVT"] = np.stack([_rel_tiles(f["KVT"], Q, 2, 128) for Q in qs])
        m["VD"] = np.stack([_rel_tiles(f["VD"], Q, 0, 1) for Q in qs])
        m["VM"] = np.stack([_rel_tiles(f["VM"], Q, 0, 1) for Q in qs])
        m["KV"] = np.stack([_rel_tiles(f["KV"], Q, 0, 1) for Q in qs])
        ik = [_rel_tiles(f["IKT"], Q, 1, 128) for Q in qs]
        m["IKT"] = np.stack([np.concatenate([a, a], axis=0) for a in ik])
        tm = np.zeros((128, 64), np.float32)
        tmI = np.zeros((128, 64), np.float32)
        tmB = np.zeros((128, 32), np.float32)
        for ch, Q in enumerate(qs):
            for u in range(32):
                if 4 * Q + 3 - u < 0:
                    tm[:, ch * 32 + u] = -30000.0
                    tmI[:, ch * 32 + u] = -1e30
            for v in range(16):
                if 2 * Q + 1 - v < 0:
                    tmB[:, ch * 16 + v] = -1e30
        m.update({"GB": GB, "c31": c31, "tm": tm, "tmI": tmI, "tmB": tmB, "lamv": lamv, "gsub": gsub,
                  "w_uv": np.ascontiguousarray(w_uv[l].reshape(10 * 512, 128)), "Esel": Esel, "CM": CM})
        m.update(_consts())
        maps.append(m)
    return _run(p, maps)


def run_C(l, outsB, res_prev, w_o, ln1_g, ln1_b, w_up, w_down, ln2_g, ln2_b, w_in, w_uk, kv_norm_g):
    with_proj = (l + 1 < DEPTH)
    p = _prog("C", build_C, with_proj)
    maps = []
    for c in range(8):
        m = {"yT": np.asarray(outsB[c]["yT"]), "res_in": np.asarray(res_prev[c]["res"]),
             "w_o": w_o[l], "w_up": w_up[l], "w_down": w_down[l],
             "ln1_g": _cols(ln1_g[l]), "ln1_b": _cols(ln1_b[l]), "ln2_g": _cols(ln2_g[l]), "ln2_b": _cols(ln2_b[l])}
        m.update(_consts())
        if with_proj:
            m.update(_proj_inputs(l + 1, w_in, w_uk, kv_norm_g))
        maps.append(m)
    return _run(p, maps)


def kernel(x, ln_emb_g, ln_emb_b, rel_bias, w_in, diff_lambda, diff_subln_g, kv_norm_g,
           w_uk, w_uv, w_o, ln1_g, ln1_b, w_up, w_down, ln2_g, ln2_b):
    a = {k_: np.asarray(v) for k_, v in dict(
        x=x, ln_emb_g=ln_emb_g, ln_emb_b=ln_emb_b, rel_bias=rel_bias, w_in=w_in, diff_lambda=diff_lambda,
        diff_subln_g=diff_subln_g, kv_norm_g=kv_norm_g, w_uk=w_uk, w_uv=w_uv, w_o=w_o, ln1_g=ln1_g, ln1_b=ln1_b,
        w_up=w_up, w_down=w_down, ln2_g=ln2_g, ln2_b=ln2_b).items()}
    cur = run_A(a["x"], a["ln_emb_g"], a["ln_emb_b"], a["w_in"], a["w_uk"], a["kv_norm_g"])
    for l in range(DEPTH):
        ob = run_B(l, cur, a["rel_bias"], a["diff_lambda"], a["diff_subln_g"], a["w_uv"])
        cur = run_C(l, ob, cur, a["w_o"], a["ln1_g"], a["ln1_b"], a["w_up"], a["w_down"], a["ln2_g"], a["ln2_b"],
                    a["w_in"], a["w_uk"], a["kv_norm_g"])
    out = np.zeros((NB, T, D), np.float32)
    for c in range(8):
        b, qs = _chunks(c)
        r = np.asarray(cur[c]["res"])
        for li, Q in enumerate(qs):
            out[b, Q * 512:(Q + 1) * 512, :] = r[li].reshape(D, 512).T
    return out
```

```python
import math
import numpy as np
import ml_dtypes
from contextlib import ExitStack
import concourse.bass as bass
import concourse.mybir as mybir
from concourse.bass_utils import run_bass_kernel_spmd

F32 = mybir.dt.float32
BF16 = mybir.dt.bfloat16
AF = mybir.ActivationFunctionType
ALU = mybir.AluOpType
AX = mybir.AxisListType
NPBF = ml_dtypes.bfloat16

D = 4096
T = 4096
NB = 2
DEPTH = 2
DIN = 12384
DFF = 16384
ALPHA = (2.0 * DEPTH) ** 0.25
LN_EPS = 1e-5
NEGM = -32768.0
SCALE = 128 ** -0.5
O_DQ, O_DK, O_DV, O_MQ, O_MK, O_MV, O_CQ, O_CKV, O_IQ, O_IK, O_IW = (
    0, 1536, 3072, 4608, 5888, 7168, 8448, 9728, 10240, 12288, 12352)

SAME_ENGINE_SYNC = True


class Buf:
    __slots__ = ("name", "w", "r")

    def __init__(self, name=""):
        self.name = name
        self.w = None
        self.r = []


class K:
    def __init__(self, nc, es):
        self.nc = nc
        self.es = es
        self.eng = {"pe": nc.tensor, "act": nc.scalar, "dve": nc.vector, "pool": nc.gpsimd, "sp": nc.sync}
        self.sems = {}
        self.cnt = {}
        self.cur = {}
        self.waited = {e: {} for e in self.eng}
        self.nsem = 0
        for e in self.eng:
            self.new_phase_sem(e)

    def _mk_sem(self, key):
        h = self.es.enter_context(self.nc.semaphore("s%d" % self.nsem))
        self.nsem += 1
        self.sems[key] = h
        self.cnt[key] = 0
        return h

    def new_phase_sem(self, e):
        key = "%s#%d" % (e, self.nsem)
        self._mk_sem(key)
        self.cur[e] = key

    def dma_sem(self, name):
        key = "dma_" + name
        if key not in self.sems:
            self._mk_sem(key)
        return key

    def _deps(self, e, reads, writes):
        deps = {}

        def add(d):
            if d is None:
                return
            sk, c, de = d
            if de == e and (e == "pe" or not SAME_ENGINE_SYNC):
                return
            if de == "dma":
                c = self.cnt[sk]
            if deps.get(sk, 0) < c:
                deps[sk] = c

        for b in reads:
            add(b.w)
        for b in writes:
            add(b.w)
            for d in b.r:
                add(d)
        out = []
        wd = self.waited[e]
        for sk, c in deps.items():
            if wd.get(sk, 0) >= c:
                continue
            wd[sk] = c
            out.append((sk, c))
        return out

    def _record(self, tag, reads, writes):
        for b in reads:
            b.r.append(tag)
            if len(b.r) > 64:
                b.r = b.r[-48:]
        for b in writes:
            b.w = tag
            b.r = []

    def _bump(self, e):
        sk = self.cur[e]
        self.cnt[sk] += 1
        tag = (sk, self.cnt[sk], e)
        if self.cnt[sk] >= 30000:
            self.new_phase_sem(e)
        return sk, tag

    def op(self, e, fn, reads=(), writes=()):
        deps = self._deps(e, reads, writes)
        eng = self.eng[e]
        for sk, c in deps[1:]:
            eng.wait_ge(self.sems[sk], c)
        ins = fn(eng)
        if deps:
            ins._wait_ge(self.sems[deps[0][0]], deps[0][1])
        sk, tag = self._bump(e)
        ins.then_inc(self.sems[sk], 1)
        self._record(tag, reads, writes)
        return ins

    def pe_group(self, fns, reads=(), writes=()):
        e = "pe"
        deps = self._deps(e, reads, writes)
        eng = self.eng[e]
        for sk, c in deps[1:]:
            eng.wait_ge(self.sems[sk], c)
        ins = None
        for i, fn in enumerate(fns):
            ins = fn(eng)
            if i == 0 and deps:
                ins._wait_ge(self.sems[deps[0][0]], deps[0][1])
        sk, tag = self._bump(e)
        ins.then_inc(self.sems[sk], 1)
        self._record(tag, reads, writes)
        return ins

    def dma(self, q, out, in_, reads=(), writes=(), sem="d"):
        deps = self._deps(q, reads, writes)
        eng = self.eng[q]
        for sk, c in deps:
            eng.wait_ge(self.sems[sk], c)
        sk = self.dma_sem(sem)
        ins = eng.dma_start(out=out, in_=in_)
        self.cnt[sk] += 16
        ins.then_inc(self.sems[sk], 16)
        self._record((sk, self.cnt[sk], "dma"), reads, writes)
        return ins

    def barrier(self):
        for e, eng in self.eng.items():
            wd = self.waited[e]
            for sk, c in self.cnt.items():
                if c > 0 and wd.get(sk, 0) < c:
                    eng.wait_ge(self.sems[sk], c)
                    wd[sk] = c

    def finish(self):
        for sk, c in self.cnt.items():
            if sk.startswith("dma_") and c > 0:
                self.eng["sp"].wait_ge(self.sems[sk], c)


class Prog:
    def __init__(self):
        self.nc = bass.Bass("TRN2", target_bir_lowering=False)
        self.es = ExitStack()
        self.k = K(self.nc, self.es)
        self.ins = []
        self.outs = []
        self.ps = []
        self.ps_i = 0
        self._st_i = 0
        self.phase = 0
        self.pes = self.es.enter_context(ExitStack())
        self._din = {}

    def next_phase(self):
        self.k.barrier()
        self.pes.close()
        self.pes = self.es.enter_context(ExitStack())
        self.phase += 1
        self._st_i = 0

    def din(self, name, shape, dt):
        if name in self._din:
            return self._din[name]
        self.ins.append(name)
        ap = self.nc.dram_tensor(name, list(shape), dt, kind="ExternalInput").ap()
        self._din[name] = ap
        return ap

    def dout(self, name, shape, dt):
        self.outs.append(name)
        return self.nc.dram_tensor(name, list(shape), dt, kind="ExternalOutput").ap()

    def sb(self, name, shape, dt):
        return self.pes.enter_context(self.nc.sbuf_tensor("ph%d_%s" % (self.phase, name), list(shape), dt))

    def mk_psum(self):
        if self.ps:
            return
        for i in range(8):
            t = self.es.enter_context(self.nc.psum_tensor("ps%d" % i, [128, 512], F32))
            self.ps.append((t, Buf("ps%d" % i)))

    def ps_next(self, lo=0, hi=8):
        i = lo + (self.ps_i % (hi - lo))
        self.ps_i += 1
        return self.ps[i]

    def mk_consts(self, ident_d, ones_d):
        k = self.k
        self.identf = self.sb("identf", [128, 128], F32)
        self.identb = self.sb("identb", [128, 128], BF16)
        self.onesf = self.sb("onesf", [128, 128], F32)
        self.onesb = self.sb("onesb", [128, 128], BF16)
        self.C = Buf("consts")
        k.dma("sp", self.identf[:], ident_d[:, :], writes=[self.C], sem="c0")
        k.dma("sp", self.onesf[:], ones_d[:, :], writes=[self.C], sem="c0")
        k.op("dve", lambda e: e.tensor_copy(out=self.identb[:], in_=self.identf[:]), reads=[self.C], writes=[self.C])
        k.op("dve", lambda e: e.tensor_copy(out=self.onesb[:], in_=self.onesf[:]), reads=[self.C], writes=[self.C])

    def mk_stage(self, n=4):
        self.stg = [(self.sb("stg%d" % i, [128, 512], BF16), Buf("stg%d" % i)) for i in range(n)]
        self.stgf = [(self.sb("stgf%d" % i, [128, 512], F32), Buf("stgf%d" % i)) for i in range(3)]
        self._stf_i = 0

    def stage(self):
        i = self._st_i % len(self.stg)
        self._st_i += 1
        return self.stg[i] + ("st%d" % i,)

    def stagef(self):
        i = self._stf_i % len(self.stgf)
        self._stf_i += 1
        return self.stgf[i] + ("stf%d" % i,)

    def mk_wslots(self, n):
        self.wsl = [(self.sb("wsl%d" % i, [128, 8192], BF16), Buf("wsl%d" % i)) for i in range(n)]
        self._w_i = 0

    def w_next(self):
        i = self._w_i % len(self.wsl)
        self._w_i += 1
        return self.wsl[i] + ("w%d" % i,)


def dense_fm(p, wview, n0, ncols, xb, XB, evac):
    k = p.k
    for g0 in range(0, ncols, 256):
        w = min(256, ncols - g0)
        slot, SL, sname = p.w_next()
        sv = slot[:, 0:32 * w].rearrange("p (a b) -> p a b", b=w)
        k.dma("pool", sv, wview[:, :, n0 + g0:n0 + g0 + w], writes=[SL], sem=sname)
        for c in range(0, w, 128):
            ps, PS = p.ps_next()
            k.pe_group([(lambda e, kc=kc, c=c, ps=ps, sv=sv: e.matmul(ps[:], lhsT=sv[:, kc, c:c + 128], rhs=xb[:, kc, :],
                                                                      start=(kc == 0), stop=(kc == 31)))
                        for kc in range(32)], reads=[SL, XB], writes=[PS])
            evac((g0 + c) // 128, ps, PS)


def dense_tm(p, wview, n0, ncols, xb, XB, evac):
    k = p.k
    for g0 in range(0, ncols, 256):
        w = min(256, ncols - g0)
        slot, SL, sname = p.w_next()
        sv = slot[:, 0:32 * w].rearrange("p (a b) -> p a b", b=w)
        k.dma("pool", sv, wview[:, :, n0 + g0:n0 + g0 + w], writes=[SL], sem=sname)
        for tb in range(4):
            ps, PS = p.ps_next()
            k.pe_group([(lambda e, kc=kc, tb=tb, ps=ps, sv=sv, w=w: e.matmul(ps[:, 0:w], lhsT=xb[:, kc, tb * 128:(tb + 1) * 128],
                                                                             rhs=sv[:, kc, 0:w], start=(kc == 0), stop=(kc == 31)))
                        for kc in range(32)], reads=[SL, XB], writes=[PS])
            evac(tb, g0, w, ps, PS)


def layernorm_fm(p, acc, ACC, xb, XB, gcol, bcol, GB):
    k = p.k
    psm, PSM = p.ps[6]
    psq, PSQ = p.ps[7]
    k.pe_group([(lambda e, kc=kc: e.matmul(psm[:], lhsT=p.onesf[:], rhs=acc[:, kc, :], start=(kc == 0), stop=(kc == 31)))
                for kc in range(32)], reads=[ACC, p.C], writes=[PSM])
    for kc in range(32):
        sq, SQ, _ = p.stagef()
        k.op("act", lambda e, kc=kc, sq=sq: e.activation(out=sq[:], in_=acc[:, kc, :], func=AF.Square), reads=[ACC], writes=[SQ])
        k.pe_group([lambda e, kc=kc, sq=sq: e.matmul(psq[:], lhsT=p.onesf[:], rhs=sq[:], start=(kc == 0), stop=(kc == 31))],
                   reads=[SQ, p.C], writes=[PSQ])
    mu, rstd, MS = p.ln_mu, p.ln_rstd, p.LNS
    k.op("dve", lambda e: e.tensor_scalar(out=mu[:], in0=psm[:], scalar1=1.0 / D, scalar2=None, op0=ALU.mult), reads=[PSM], writes=[MS])
    k.op("dve", lambda e: e.tensor_tensor(out=rstd[:], in0=mu[:], in1=mu[:], op=ALU.mult), reads=[MS], writes=[MS])
    k.op("dve", lambda e: e.scalar_tensor_tensor(out=rstd[:], in0=psq[:], scalar=1.0 / D, in1=rstd[:], op0=ALU.mult, op1=ALU.subtract),
         reads=[PSQ, MS], writes=[MS])
    k.op("dve", lambda e: e.tensor_scalar(out=rstd[:], in0=rstd[:], scalar1=LN_EPS, scalar2=None, op0=ALU.add), reads=[MS], writes=[MS])
    k.op("act", lambda e: e.activation(out=rstd[:], in_=rstd[:], func=AF.Sqrt), reads=[MS], writes=[MS])
    k.op("dve", lambda e: e.reciprocal(out=rstd[:], in_=rstd[:]), reads=[MS], writes=[MS])
    for kc in range(32):
        k.op("dve", lambda e, kc=kc: e.tensor_tensor(out=acc[:, kc, :], in0=acc[:, kc, :], in1=mu[:], op=ALU.subtract), reads=[ACC, MS], writes=[ACC])
        k.op("dve", lambda e, kc=kc: e.tensor_tensor(out=acc[:, kc, :], in0=acc[:, kc, :], in1=rstd[:], op=ALU.mult), reads=[ACC, MS], writes=[ACC])
        k.op("act", lambda e, kc=kc: e.activation(out=acc[:, kc, :], in_=acc[:, kc, :], func=AF.Identity,
                                                  bias=bcol[:, kc:kc + 1], scale=gcol[:, kc:kc + 1]), reads=[ACC, GB], writes=[ACC])
        k.op("act", lambda e, kc=kc: e.activation(out=xb[:, kc, :], in_=acc[:, kc, :], func=AF.Copy), reads=[ACC], writes=[XB])


def load_cols(p, name, dram, ncol):
    t = p.sb(name, [128, ncol], F32)
    B = Buf(name)
    p.k.dma("sp", t[:], dram[:, :], writes=[B], sem="c0")
    return t, B


def declare_proj_outputs(p):
    o = {}
    o["qd"] = p.dout("qd", [2, 12, 128, 512], BF16)
    o["kd"] = p.dout("kd", [2, 12, 128, 512], BF16)
    o["vd"] = p.dout("vd", [2, 4, 128, 1536], BF16)
    o["qm"] = p.dout("qm", [2, 10, 128, 512], BF16)
    o["km"] = p.dout("km", [2, 10, 128, 512], BF16)
    o["vm"] = p.dout("vm", [2, 4, 128, 1280], BF16)
    o["ql"] = p.dout("ql", [2, 40, 128, 512], BF16)
    o["kvT"] = p.dout("kvT", [2, 4, 128, 512], BF16)
    o["kv"] = p.dout("kv", [2, 4, 128, 512], BF16)
    o["qi"] = p.dout("qi", [2, 16, 128, 512], BF16)
    o["ikT"] = p.dout("ikT", [2, 64, 512], BF16)
    o["iw"] = p.dout("iw", [2, 4, 128, 32], F32)
    return o


def declare_proj_inputs(p):
    i = {}
    i["w_in"] = p.din("w_in", [D, DIN], F32)
    i["w_uk"] = p.din("w_uk", [10 * 512, 128], F32)
    i["gkv"] = p.din("gkv", [128, 512], F32)
    return i


def proj_setup(p, pi, acc, ACC):
    k = p.k
    p.wukT = p.sb("wukT", [128, 10, 512], BF16)
    p.WUK = Buf("wukT")
    p.gkv, p.GKV = load_cols(p, "gkv_s", pi["gkv"], 512)
    wuk_v = pi["w_uk"].rearrange("(a p) d -> p a d", p=128)
    for a0 in range(0, 40, 4):
        st, ST, sname = p.stagef()
        k.dma("sp", st[:].rearrange("p (a d) -> p a d", d=128), wuk_v[:, a0:a0 + 4, :], writes=[ST], sem=sname)
        ps, PS = p.ps_next()
        for a in range(4):
            k.op("pe", lambda e, a=a, ps=ps, st=st: e.transpose(out=ps[:, a * 128:(a + 1) * 128], in_=st[:, a * 128:(a + 1) * 128],
                                                                identity=p.identf[:]), reads=[ST, p.C], writes=[PS])
        h, cc0 = divmod(a0, 4)
        k.op("act", lambda e, ps=ps, h=h: e.activation(out=p.wukT[:, h, :], in_=ps[:], func=AF.Copy), reads=[PS], writes=[p.WUK])
    p.cqT = acc[:, 0:5, :].rearrange("p a t -> p (a t)").bitcast(BF16).rearrange("p (h t) -> p h t", t=512)
    p.CQ = ACC
    p.ckvr = acc[:, 8:12, :]
    p.CKV = ACC
    p.sm = p.sb("sm_small", [128, 16], F32)
    p.SM = Buf("sm")
    p.ikc = p.sb("ikc", [128, 128], F32)
    p.IKC = Buf("ikc")


def proj_chunk(p, pi, po, ch, xb, XB):
    k = p.k
    wv = pi["w_in"].rearrange("(kc p) n -> p kc n", p=128)

    def store_fm(dst, scale):
        def ev(ci, ps, PS):
            st, ST, sname = p.stage()
            k.op("act", lambda e: e.activation(out=st[:], in_=ps[:], func=AF.Copy, scale=scale), reads=[PS], writes=[ST])
            k.dma("sp", dst[ch, ci], st[:], reads=[ST], sem=sname)
        return ev

    dense_fm(p, wv, O_DQ, 1536, xb, XB, store_fm(po["qd"], SCALE))
    dense_fm(p, wv, O_DK, 1536, xb, XB, store_fm(po["kd"], 1.0))
    dense_fm(p, wv, O_MQ, 1280, xb, XB, store_fm(po["qm"], SCALE))
    dense_fm(p, wv, O_MK, 1280, xb, XB, store_fm(po["km"], 1.0))
    dense_fm(p, wv, O_IQ, 2048, xb, XB, store_fm(po["qi"], 0.125))

    def ev_cq(ci, ps, PS):
        k.op("act", lambda e: e.activation(out=p.cqT[:, ci, :], in_=ps[:], func=AF.Copy), reads=[PS], writes=[p.CQ])
    dense_fm(p, wv, O_CQ, 1280, xb, XB, ev_cq)
    for h in range(10):
        for cc in range(4):
            ps, PS = p.ps_next()
            k.pe_group([lambda e, h=h, cc=cc, ps=ps: e.matmul(ps[:], lhsT=p.wukT[:, h, cc * 128:(cc + 1) * 128], rhs=p.cqT[:, h, :],
                                                             start=True, stop=True)], reads=[p.WUK, p.CQ], writes=[PS])
            store_fm(po["ql"], SCALE)(h * 4 + cc, ps, PS)

    def store_tm(dst):
        def ev(tb, g0, w, ps, PS):
            st, ST, sname = p.stage()
            k.op("act", lambda e: e.activation(out=st[:, 0:w], in_=ps[:, 0:w], func=AF.Copy), reads=[PS], writes=[ST])
            k.dma("sp", dst[ch, tb, :, g0:g0 + w], st[:, 0:w], reads=[ST], sem=sname)
        return ev

    dense_tm(p, wv, O_DV, 1536, xb, XB, store_tm(po["vd"]))
    dense_tm(p, wv, O_MV, 1280, xb, XB, store_tm(po["vm"]))

    def ev_ckv(tb, g0, w, ps, PS):
        k.op("act", lambda e: e.activation(out=p.ckvr[:, tb, g0:g0 + w], in_=ps[:, 0:w], func=AF.Copy), reads=[PS], writes=[p.CKV])
    dense_tm(p, wv, O_CKV, 512, xb, XB, ev_ckv)
    sm = p.sm
    for tb in range(4):
        sq, SQ, _ = p.stagef()
        k.op("act", lambda e, tb=tb, sq=sq: e.activation(out=sq[:], in_=p.ckvr[:, tb, :], func=AF.Square), reads=[p.CKV], writes=[SQ])
        k.op("dve", lambda e, sq=sq: e.reduce_sum(out=sm[:, 0:1], in_=sq[:], axis=AX.X), reads=[SQ], writes=[p.SM])
        k.op("dve", lambda e: e.tensor_scalar(out=sm[:, 0:1], in0=sm[:, 0:1], scalar1=1.0 / 512, scalar2=1e-5, op0=ALU.mult, op1=ALU.add),
             reads=[p.SM], writes=[p.SM])
        k.op("act", lambda e: e.activation(out=sm[:, 0:1], in_=sm[:, 0:1], func=AF.Sqrt), reads=[p.SM], writes=[p.SM])
        k.op("dve", lambda e: e.reciprocal(out=sm[:, 0:1], in_=sm[:, 0:1]), reads=[p.SM], writes=[p.SM])
        k.op("dve", lambda e, tb=tb: e.scalar_tensor_tensor(out=p.ckvr[:, tb, :], in0=p.ckvr[:, tb, :], scalar=sm[:, 0:1], in1=p.gkv[:],
                                                            op0=ALU.mult, op1=ALU.mult), reads=[p.CKV, p.SM, p.GKV], writes=[p.CKV])
        st, ST, sname = p.stage()
        k.op("act", lambda e, tb=tb, st=st: e.activation(out=st[:], in_=p.ckvr[:, tb, :], func=AF.Copy), reads=[p.CKV], writes=[ST])
        k.dma("sp", po["kv"][ch, tb], st[:], reads=[ST], sem=sname)
        ps, PS = p.ps_next()
        for cc in range(4):
            k.op("pe", lambda e, cc=cc, tb=tb, ps=ps: e.transpose(out=ps[:, cc * 128:(cc + 1) * 128], in_=p.ckvr[:, tb, cc * 128:(cc + 1) * 128],
                                                                  identity=p.identf[:]), reads=[p.CKV, p.C], writes=[PS])
        st, ST, sname = p.stage()
        k.op("act", lambda e, st=st, ps=ps: e.activation(out=st[:], in_=ps[:], func=AF.Copy), reads=[PS], writes=[ST])
        k.dma("sp", po["kvT"][ch, :, :, tb * 128:(tb + 1) * 128].rearrange("c p t -> p c t"),
              st[:].rearrange("p (c t) -> p c t", t=128), reads=[ST], sem=sname)

    def ev_ik(tb, g0, w, ps, PS):
        ikc = p.ikc
        k.op("dve", lambda e: e.reduce_sum(out=sm[:, 1:2], in_=ps[:, 0:64], axis=AX.X), reads=[PS], writes=[p.SM])
        k.op("dve", lambda e: e.tensor_scalar(out=sm[:, 1:2], in0=sm[:, 1:2], scalar1=1.0 / 64, scalar2=None, op0=ALU.mult), reads=[p.SM], writes=[p.SM])
        k.op("dve", lambda e: e.tensor_scalar(out=ikc[:, 0:64], in0=ps[:, 0:64], scalar1=sm[:, 1:2], scalar2=None, op0=ALU.subtract),
             reads=[PS, p.SM], writes=[p.IKC])
        k.op("dve", lambda e: e.tensor_tensor(out=ikc[:, 64:128], in0=ikc[:, 0:64], in1=ikc[:, 0:64], op=ALU.mult), reads=[p.IKC], writes=[p.IKC])
        k.op("dve", lambda e: e.reduce_sum(out=sm[:, 2:3], in_=ikc[:, 64:128], axis=AX.X), reads=[p.IKC], writes=[p.SM])
        k.op("dve", lambda e: e.tensor_scalar(out=sm[:, 2:3], in0=sm[:, 2:3], scalar1=1.0 / 64, scalar2=LN_EPS, op0=ALU.mult, op1=ALU.add),
             reads=[p.SM], writes=[p.SM])
        k.op("act", lambda e: e.activation(out=sm[:, 2:3], in_=sm[:, 2:3], func=AF.Sqrt), reads=[p.SM], writes=[p.SM])
        k.op("dve", lambda e: e.reciprocal(out=sm[:, 2:3], in_=sm[:, 2:3]), reads=[p.SM], writes=[p.SM])
        k.op("dve", lambda e: e.tensor_scalar(out=ikc[:, 0:64], in0=ikc[:, 0:64], scalar1=sm[:, 2:3], scalar2=None, op0=ALU.mult),
             reads=[p.IKC, p.SM], writes=[p.IKC])
        st, ST, sname = p.stagef()
        k.op("dve", lambda e: e.tensor_scalar(out=st[:, 0:32], in0=ps[:, 64:96], scalar1=32 ** -0.5, scalar2=None, op0=ALU.mult),
             reads=[PS], writes=[ST])
        k.dma("sp", po["iw"][ch, tb], st[:, 0:32], reads=[ST], sem=sname)
        ps2, PS2 = p.ps_next()
        k.op("pe", lambda e: e.transpose(out=ps2[0:64, 0:128], in_=ikc[:, 0:64], identity=p.identf[:]), reads=[p.IKC, p.C], writes=[PS2])
        st2, ST2, sname2 = p.stage()
        k.op("act", lambda e: e.activation(out=st2[0:64, 0:128], in_=ps2[0:64, 0:128], func=AF.Copy), reads=[PS2], writes=[ST2])
        k.dma("sp", po["ikT"][ch, :, tb * 128:(tb + 1) * 128], st2[0:64, 0:128], reads=[ST2], sem=sname2)
    dense_tm(p, wv, O_IK, 96, xb, XB, ev_ik)


def build_A():
    p = Prog()
    k = p.k
    xT = p.din("xT", [2, 32, 128, 512], F32)
    g_d = p.din("ln_g", [128, 32], F32)
    b_d = p.din("ln_b", [128, 32], F32)
    ident_d = p.din("ident", [128, 128], F32)
    ones_d = p.din("ones", [128, 128], F32)
    pi = declare_proj_inputs(p)
    po = declare_proj_outputs(p)
    res = p.dout("res", [2, 32, 128, 512], F32)
    p.mk_psum()
    p.mk_consts(ident_d, ones_d)
    p.mk_stage()
    p.mk_wslots(4)
    acc = p.sb("acc", [128, 32, 512], F32)
    ACC = Buf("acc")
    xb = p.sb("xb", [128, 32, 512], BF16)
    XB = Buf("xb")
    p.ln_mu = p.sb("ln_mu", [128, 512], F32)
    p.ln_rstd = p.sb("ln_rstd", [128, 512], F32)
    p.LNS = Buf("lns")
    gcol, G1 = load_cols(p, "gcol", g_d, 32)
    bcol, G2 = load_cols(p, "bcol", b_d, 32)
    GB = Buf("gb")
    k.op("dve", lambda e: e.tensor_copy(out=gcol[:], in_=gcol[:]), reads=[G1, G2], writes=[GB])
    proj_setup(p, pi, acc, ACC)
    for ch in range(2):
        for q in range(4):
            k.dma("sp", acc[:, q * 8:(q + 1) * 8, :], xT[ch, q * 8:(q + 1) * 8].rearrange("a p t -> p a t"), writes=[ACC], sem="ld%d" % q)
        layernorm_fm(p, acc, ACC, xb, XB, gcol, bcol, GB)
        for q in range(4):
            k.dma("sp", res[ch, q * 8:(q + 1) * 8].rearrange("a p t -> p a t"), acc[:, q * 8:(q + 1) * 8, :], reads=[ACC], sem="sr%d" % q)
        proj_chunk(p, pi, po, ch, xb, XB)
    k.finish()
    return p


def build_C(with_proj, p=None, yT_ap=None):
    own = p is None
    if own:
        p = Prog()
    k = p.k
    yT = yT_ap if yT_ap is not None else p.din("yT", [2, 32, 128, 512], BF16)
    res_in = p.din("res_in", [2, 32, 128, 512], F32)
    w_o = p.din("w_o", [D, D], F32)
    w_up = p.din("w_up", [D, DFF], F32)
    w_down = p.din("w_down", [DFF, D], F32)
    g1_d = p.din("ln1_g", [128, 32], F32)
    b1_d = p.din("ln1_b", [128, 32], F32)
    g2_d = p.din("ln2_g", [128, 32], F32)
    b2_d = p.din("ln2_b", [128, 32], F32)
    ident_d = p.din("ident", [128, 128], F32)
    ones_d = p.din("ones", [128, 128], F32)
    if with_proj:
        pi = declare_proj_inputs(p)
        po = declare_proj_outputs(p)
    res = p.dout("res", [2, 32, 128, 512], F32)
    p.mk_psum()
    p.mk_consts(ident_d, ones_d)
    p.mk_stage()
    p.mk_wslots(4)
    acc = p.sb("acc", [128, 32, 512], F32)
    ACC = Buf("acc")
    xb = p.sb("xb", [128, 32, 512], BF16)
    XB = Buf("xb")
    hT = [(p.sb("hT%d" % i, [128, 512], BF16), Buf("hT%d" % i)) for i in range(4)]
    p.ln_mu = p.sb("ln_mu", [128, 512], F32)
    p.ln_rstd = p.sb("ln_rstd", [128, 512], F32)
    p.LNS = Buf("lns")
    g1, A1 = load_cols(p, "g1", g1_d, 32)
    b1, A2 = load_cols(p, "b1", b1_d, 32)
    g2, A3 = load_cols(p, "g2", g2_d, 32)
    b2, A4 = load_cols(p, "b2", b2_d, 32)
    GB = Buf("gb")
    k.op("dve", lambda e: e.tensor_copy(out=g1[:], in_=g1[:]), reads=[A1, A2, A3, A4], writes=[GB])
    if with_proj:
        proj_setup(p, pi, acc, ACC)
    wo_v = w_o.rearrange("(kc p) n -> p kc n", p=128)
    wu_v = w_up.rearrange("(kc p) n -> p kc n", p=128)
    wd_v = w_down.rearrange("(a p) n -> p a n", p=128)
    accf = acc[:].rearrange("p a t -> p (a t)")
    for ch in range(2):
        for q in range(4):
            k.dma("sp", acc[:, q * 8:(q + 1) * 8, :], res_in[ch, q * 8:(q + 1) * 8].rearrange("a p t -> p a t"), writes=[ACC], sem="ld%d" % q)
            k.dma("sp", xb[:, q * 8:(q + 1) * 8, :], yT[ch, q * 8:(q + 1) * 8].rearrange("a p t -> p a t"), writes=[XB], sem="ly%d" % q)
        k.op("act", lambda e: e.activation(out=accf, in_=accf, func=AF.Copy, scale=ALPHA), reads=[ACC], writes=[ACC])

        def ev_add(ci, ps, PS):
            k.op("dve", lambda e: e.tensor_tensor(out=acc[:, ci, :], in0=ps[:], in1=acc[:, ci, :], op=ALU.add), reads=[PS, ACC], writes=[ACC])
        dense_fm(p, wo_v, 0, D, xb, XB, ev_add)
        layernorm_fm(p, acc, ACC, xb, XB, g1, b1, GB)
        k.op("act", lambda e: e.activation(out=accf, in_=accf, func=AF.Copy, scale=ALPHA), reads=[ACC], writes=[ACC])
        NG = DFF // 256
        state = {}

        def up(g):
            slot, SL, sname = p.w_next()
            sv = slot[:].rearrange("p (a b) -> p a b", b=256)
            k.dma("pool", sv, wu_v[:, :, g * 256:(g + 1) * 256], writes=[SL], sem=sname)
            hs = []
            for c in range(2):
                ps, PS = p.ps_next(0, 6)
                k.pe_group([(lambda e, kc=kc, c=c, ps=ps, sv=sv: e.matmul(ps[:], lhsT=sv[:, kc, c * 128:(c + 1) * 128], rhs=xb[:, kc, :],
                                                                          start=(kc == 0), stop=(kc == 31))) for kc in range(32)],
                           reads=[SL, XB], writes=[PS])
                rt, RT, _ = p.stagef()
                k.op("act", lambda e, rt=rt, ps=ps: e.activation(out=rt[:], in_=ps[:], func=AF.Relu), reads=[PS], writes=[RT])
                h, H = hT[(g % 2) * 2 + c]
                k.op("dve", lambda e, rt=rt, h=h: e.tensor_tensor(out=h[:], in0=rt[:], in1=rt[:], op=ALU.mult), reads=[RT], writes=[H])
                hs.append((h, H))
            state[g] = hs

        def down(g):
            slot, SL, sname = p.w_next()
            sv = slot[:].rearrange("p (a b) -> p a b", b=4096)
            k.dma("pool", sv, wd_v[:, g * 2:(g + 1) * 2, :], writes=[SL], sem=sname)
            hs = state.pop(g)
            for dc in range(32):
                ps, PS = p.ps_next(0, 6)
                k.pe_group([(lambda e, c=c, dc=dc, ps=ps, sv=sv: e.matmul(ps[:], lhsT=sv[:, c, dc * 128:(dc + 1) * 128], rhs=hs[c][0][:],
                                                                          start=(c == 0), stop=(c == 1))) for c in range(2)],
                           reads=[SL, hs[0][1], hs[1][1]], writes=[PS])
                ev_add(dc, ps, PS)

        up(0)
        for g in range(NG):
            if g + 1 < NG:
                up(g + 1)
            down(g)
        layernorm_fm(p, acc, ACC, xb, XB, g2, b2, GB)
        for q in range(4):
            k.dma("sp", res[ch, q * 8:(q + 1) * 8].rearrange("a p t -> p a t"), acc[:, q * 8:(q + 1) * 8, :], reads=[ACC], sem="sr%d" % q)
        if with_proj:
            proj_chunk(p, pi, po, ch, xb, XB)
    if own:
        k.finish()
    return p


_CACHE = {}


def _prog(name, fn, *a):
    key = (name,) + a
    if key not in _CACHE:
        _CACHE[key] = fn(*a)
    return _CACHE[key]


def _chunks(c):
    r = c % 4
    return c // 4, (r, 7 - r)


def _fm(a2d):
    return np.ascontiguousarray(a2d.T).reshape(32, 128, 512)


def _cols(v):
    return np.ascontiguousarray(v.reshape(-1, 128).T)


def _consts():
    return {"ident": np.eye(128, dtype=np.float32), "ones": np.ones((128, 128), np.float32)}


def _proj_inputs(l, w_in, w_uk, kv_norm_g):
    return {"w_in": w_in[l], "w_uk": np.ascontiguousarray(w_uk[l].reshape(10 * 512, 128)),
            "gkv": np.ascontiguousarray(np.broadcast_to(kv_norm_g[l][None, :], (128, 512)))}


def _run(p, in_maps):
    res = run_bass_kernel_spmd(p.nc, in_maps, core_ids=list(range(8)))
    return res.results


def run_A(x, ln_emb_g, ln_emb_b, w_in, w_uk, kv_norm_g):
    p = _prog("A", build_A)
    maps = []
    for c in range(8):
        b, qs = _chunks(c)
        m = {"xT": np.stack([_fm(x[b, q * 512:(q + 1) * 512, :]) for q in qs]),
             "ln_g": _cols(ln_emb_g), "ln_b": _cols(ln_emb_b)}
        m.update(_consts())
        m.update(_proj_inputs(0, w_in, w_uk, kv_norm_g))
        maps.append(m)
    return _run(p, maps)


UCH = (16, 32)


def build_B(layer_idx, p=None, yT_ap=None):
    lam_init = 0.8 - 0.6 * math.exp(-0.3 * layer_idx)
    own = p is None
    if own:
        p = Prog()
    k = p.k
    qd = p.din("b_qd", [2, 12, 128, 512], BF16)
    qm = p.din("b_qm", [2, 10, 128, 512], BF16)
    ql = p.din("b_ql", [2, 40, 128, 512], BF16)
    qi = p.din("b_qi", [2, 16, 128, 512], BF16)
    iw = p.din("b_iw", [2, 4, 128, 32], F32)
    KD = p.din("KD", [2, 12, 128, 4096], BF16)
    VD = p.din("VD", [2, 32, 128, 1536], BF16)
    KM = p.din("KM", [2, 10, 128, 4096], BF16)
    VM = p.din("VM", [2, 32, 128, 1280], BF16)
    KVT = p.din("KVT", [2, 4, 128, 4096], BF16)
    KV = p.din("KV", [2, 32, 128, 512], BF16)
    IKT = p.din("IKT", [2, 128, 4096], BF16)
    GBd = p.din("GB", [32, 128, 1024], F32)
    c31d = p.din("c31", [128, 32], F32)
    tmd = p.din("tm", [128, 64], F32)
    tmId = p.din("tmI", [128, 64], F32)
    tmBd = p.din("tmB", [128, 32], F32)
    lamd = p.din("lamv", [128, 512], F32)
    gsd = p.din("gsub", [128, 2], F32)
    wuvd = p.din("w_uv", [10 * 512, 128], F32)
    Ed = p.din("Esel", [16, 2048], F32)
    CMd = p.din("CM", [128, 128], F32)
    ident_d = p.din("ident", [128, 128], F32)
    ones_d = p.din("ones", [128, 128], F32)
    yT = yT_ap if yT_ap is not None else p.dout("yT", [2, 32, 128, 512], BF16)
    p.mk_psum()
    p.mk_consts(ident_d, ones_d)
    p.mk_stage(4)
    c31, C31 = load_cols(p, "c31s", c31d, 32)
    tm, TM = load_cols(p, "tms", tmd, 64)
    tmI, TMI = load_cols(p, "tmIs", tmId, 64)
    tmB, TMB = load_cols(p, "tmBs", tmBd, 32)
    CM, CMB = load_cols(p, "CMs", CMd, 128)
    gs, GS = load_cols(p, "gss", gsd, 2)
    KA = p.sb("KA", [128, 16384], BF16)
    VA = p.sb("VA", [128, 16384], BF16)
    KAb = [Buf("KA0"), Buf("KA1")]
    VAb = [Buf("VA0"), Buf("VA1")]
    nmT = p.sb("nmT", [128, 32, 512], BF16)
    NMT = Buf("nmT")
    sc = p.sb("sc", [128, 4096], F32)
    SC = Buf("sc")
    wk = p.sb("wk", [128, 4096], F32)
    WK = Buf("wk")
    m8 = p.sb("m8", [128, 8], F32)
    M8 = Buf("m8")
    m8b = p.sb("m8b", [128, 8], F32)
    M8B = Buf("m8b")
    iqs = [(p.sb("iq%d" % i, [128, 16, 128], BF16), Buf("iq%d" % i)) for i in range(2)]
    ikt = p.sb("ikt", [128, 4096], BF16)
    IKB = Buf("ikt")
    pts = [(p.sb("pt%d" % i, [128, 512], BF16), Buf("pt%d" % i)) for i in range(3)]
    rts = p.stgf
    SCg = [Buf("sc%d" % i) for i in range(8)]
    qhs = [(p.sb("qh%d" % i, [128, 2048], BF16), Buf("qh%d" % i)) for i in range(2)]
    ghs = [(p.sb("gh%d" % i, [128, 1024], BF16), Buf("gh%d" % i)) for i in range(2)]
    om = p.sb("om", [128, 4, 512], F32)
    OM = Buf("om")
    olb = p.sb("olb", [128, 4, 512], BF16)
    OLB = Buf("olb")
    rden = p.sb("rden", [128, 512], F32)
    RD = Buf("rden")
    wuv = p.sb("wuv", [128, 40, 128], BF16)
    WUV = Buf("wuv")
    Es = p.sb("Es", [16, 2048], BF16)
    ESB = Buf("Es")
    fb = p.sb("fb", [128, 64], F32)
    FB = Buf("fb")
    sm = p.sb("smB", [128, 64], F32)
    SM = Buf("smB")
    wsg = p.sb("wsg", [128, 64], F32)
    WSG = Buf("wsg")
    gt = p.sb("gt", [128, 16], F32)
    GT = Buf("gt")
    snT = p.sb("snT", [16, 512], BF16)
    SNT = Buf("snT")
    kmT = p.sb("kmT", [128, 16], F32)
    kmTb = p.sb("kmTb", [128, 16], BF16)
    KMT = Buf("kmT")
    k.dma("pool", wuv[:], wuvd.rearrange("(a p) d -> p a d", p=128), writes=[WUV], sem="cw")
    k.dma("pool", Es[:], Ed[:, :], writes=[ESB], sem="cw")
    lamt, LAMT = wk, WK
    k.dma("sp", wk[:, 0:512], lamd[:, :], writes=[WK], sem="c0")
    k.op("dve", lambda e: e.tensor_tensor(out=sc[:, 0:128], in0=lamt[:, 0:128], in1=lamt[:, 128:256], op=ALU.mult), reads=[LAMT], writes=[SC])
    k.op("dve", lambda e: e.tensor_tensor(out=sc[:, 128:256], in0=lamt[:, 256:384], in1=lamt[:, 384:512], op=ALU.mult), reads=[LAMT], writes=[SC])
    k.op("dve", lambda e: e.reduce_sum(out=sm[:, 1:2], in_=sc[:, 0:128], axis=AX.X), reads=[SC], writes=[SM])
    k.op("dve", lambda e: e.reduce_sum(out=sm[:, 2:3], in_=sc[:, 128:256], axis=AX.X), reads=[SC], writes=[SM])
    k.op("act", lambda e: e.activation(out=sm[:, 1:3], in_=sm[:, 1:3], func=AF.Exp), reads=[SM], writes=[SM])
    k.op("dve", lambda e: e.scalar_tensor_tensor(out=sm[:, 0:1], in0=sm[:, 2:3], scalar=-lam_init, in1=sm[:, 1:2], op0=ALU.add, op1=ALU.subtract),
         reads=[SM], writes=[SM])
    k.op("dve", lambda e: e.tensor_scalar(out=gs[:], in0=gs[:], scalar1=1.0 - lam_init, scalar2=None, op0=ALU.mult), reads=[GS], writes=[GS])

    st = {"s": 0, "pt": 0, "q": 0, "g": 0, "ka": 0, "va": 0, "rt": 0}

    def rot(lst, key):
        i = st[key] % len(lst)
        st[key] += 1
        return lst[i] + (i,)

    def load_bias(col, ch):
        gh, GH, gi = rot(ghs, "g")
        k.dma("pool", gh[:], GBd[col], writes=[GH], sem="g%d" % gi)
        k.op("dve", lambda e: e.tensor_scalar(out=fb[:, 0:32], in0=tm[:, ch * 32:(ch + 1) * 32], scalar1=c31[:, col:col + 1], scalar2=None, op0=ALU.add),
             reads=[TM, C31], writes=[FB])
        return gh, GH

    def attn_loop(ch, U, s_parts, s_reads, gh, GH, av_parts, av_reads, nacc):
        accB = [p.ps[i][1] for i in range(nacc)] + [p.ps[4][1]]

        def qk(u):
            S, SB = p.ps[5 + (st["s"] % 2)]
            st["s"] += 1
            parts = list(s_parts(u))
            rd = list(s_reads) + [p.C]
            if u <= 4:
                parts.append((p.identb[:], gh[:, 128 * u:128 * u + 512]))
                rd.append(GH)
            n = len(parts)
            k.pe_group([(lambda e, i=i, S=S, pr=pr: e.matmul(S[:], lhsT=pr[0], rhs=pr[1], start=(i == 0), stop=(i == n - 1)))
                        for i, pr in enumerate(parts)], reads=rd, writes=[SB])
            return S, SB

        nxt = qk(0)
        for u in range(U):
            S, SB = nxt
            if u + 1 < U:
                nxt = qk(u + 1)
            pt, PT, _ = rot(pts, "pt")
            if u <= 3:
                k.op("act", lambda e, S=S, pt=pt: e.activation(out=pt[:], in_=S[:], func=AF.Exp), reads=[SB], writes=[PT])
            elif u == 4:
                k.op("act", lambda e, S=S, pt=pt: e.activation(out=pt[:], in_=S[:], func=AF.Exp, bias=tm[:, ch * 32 + 4:ch * 32 + 5]),
                     reads=[SB, TM], writes=[PT])
            else:
                k.op("act", lambda e, S=S, pt=pt, u=u: e.activation(out=pt[:], in_=S[:], func=AF.Exp, bias=fb[:, u:u + 1]),
                     reads=[SB, FB], writes=[PT])
            av = list(av_parts(u)) + [(4, p.onesb[:])]
            k.pe_group([(lambda e, b=b, l=l, pt=pt, u=u: e.matmul(p.ps[b][0][:], lhsT=l, rhs=pt[:], start=(u == 0), stop=(u == U - 1)))
                        for b, l in av], reads=[PT, p.C] + list(av_reads), writes=accB)
        k.op("dve", lambda e: e.reciprocal(out=rden[:], in_=p.ps[4][0][:]), reads=[p.ps[4][1]], writes=[RD])

    def store_y(ch, idx, src_fn, reads):
        stg, ST, sname = p.stage()
        k.op("dve", lambda e: src_fn(e, stg), reads=reads, writes=[ST])
        k.dma("sp", yT[ch, idx], stg[:], reads=[ST], sem=sname)

    def gen_prep(ch, U, NK):
        k.dma("pool", ikt[:, 0:NK], IKT[ch, :, 0:NK], writes=[IKB], sem="ik")
        for qb in range(4):
            iq, IQ, ii = rot(iqs, "q")
            k.dma("pool", iq[:], qi[ch, :, :, qb * 128:(qb + 1) * 128].rearrange("a p t -> p a t"), writes=[IQ], sem="iq%d" % ii)
            wst, WST, wname = wsg, WSG, "wsg"
            k.dma("sp", wst[:, 0:32], iw[ch, qb], writes=[WST], sem=wname)
            k.op("act", lambda e, wst=wst: e.activation(out=sm[:, 8:40], in_=wst[:, 0:32], func=AF.Abs), reads=[WST], writes=[SM])
            k.op("act", lambda e, wst=wst: e.activation(out=wst[:, 32:64], in_=wst[:, 0:32], func=AF.Sign), reads=[WST], writes=[WST])
            for g in range(U // 4):
                for h in range(32):
                    pb = 64 * (h % 2)
                    ps, PS = p.ps_next(2, 4)
                    k.pe_group([lambda e, ps=ps, pb=pb, h=h, g=g, iq=iq: e.matmul(ps[:], lhsT=iq[pb:pb + 64, h // 2, :], rhs=ikt[pb:pb + 64, g * 512:(g + 1) * 512],
                                                                                  start=True, stop=True)], reads=[IQ, IKB], writes=[PS])
                    rt, RT, _ = rot(rts, "rt")
                    k.op("act", lambda e, rt=rt, ps=ps, h=h: e.activation(out=rt[:], in_=ps[:], func=AF.Relu, scale=sm[:, 8 + h:9 + h]),
                         reads=[PS, SM], writes=[RT])
                    eng = "dve"
                    if h == 0:
                        k.op(eng, lambda e, rt=rt, g=g, wst=wst: e.tensor_scalar(out=sc[:, g * 512:(g + 1) * 512], in0=rt[:], scalar1=wst[:, 32:33], scalar2=None,
                                                                                  op0=ALU.mult), reads=[RT, WST], writes=[SCg[g]])
                    else:
                        k.op(eng, lambda e, rt=rt, g=g, h=h, wst=wst: e.scalar_tensor_tensor(out=sc[:, g * 512:(g + 1) * 512], in0=rt[:], scalar=wst[:, 32 + h:33 + h],
                                                                                              in1=sc[:, g * 512:(g + 1) * 512], op0=ALU.mult, op1=ALU.add),
                             reads=[RT, WST, SCg[g]], writes=[SCg[g]])
                    if h % 8 == 7:
                        yield
            allsc = SCg[0:U // 4]
            if qb < 3:
                k.op("dve", lambda e, qb=qb: e.memset(sc[:, 0:(3 - qb) * 128], -1e30), writes=allsc)
            k.op("dve", lambda e, qb=qb: e.tensor_tensor(out=sc[:, (3 - qb) * 128:(4 - qb) * 128], in0=sc[:, (3 - qb) * 128:(4 - qb) * 128], in1=CM[:], op=ALU.add),
                 reads=allsc + [CMB], writes=allsc)
            for u in range(4, U):
                k.op("dve", lambda e, u=u: e.tensor_scalar(out=sc[:, u * 128:(u + 1) * 128], in0=sc[:, u * 128:(u + 1) * 128], scalar1=tmI[:, ch * 32 + u:ch * 32 + u + 1],
                                                           scalar2=None, op0=ALU.add), reads=allsc + [TMI], writes=allsc)
            NIT = 26
            k.op("dve", lambda e: e.memset(m8[:, 0:1], -64.0), writes=[M8])
            for it in range(NIT):
                halfw = 128.0 / 2 ** (it + 1)
                k.op("dve", lambda e, halfw=halfw: e.tensor_scalar(out=m8[:, 1:2], in0=m8[:, 0:1], scalar1=halfw, scalar2=None, op0=ALU.add), reads=[M8], writes=[M8])
                k.op("dve", lambda e: e.tensor_scalar(out=wk[:, 0:NK], in0=sc[:, 0:NK], scalar1=m8[:, 1:2], scalar2=0.0, op0=ALU.is_ge, op1=ALU.add,
                                                      accum_out=m8[:, 2:3]), reads=allsc + [M8], writes=[WK, M8])
                k.op("dve", lambda e, halfw=halfw: e.tensor_scalar(out=m8[:, 3:4], in0=m8[:, 2:3], scalar1=255.5, scalar2=halfw, op0=ALU.is_ge, op1=ALU.mult),
                     reads=[M8], writes=[M8])
                k.op("dve", lambda e: e.tensor_tensor(out=m8[:, 0:1], in0=m8[:, 0:1], in1=m8[:, 3:4], op=ALU.add), reads=[M8], writes=[M8])
                if it % 3 == 2:
                    yield
            k.op("dve", lambda e: e.tensor_copy(out=m8[:, 7:8], in_=m8[:, 0:1]), reads=[M8], writes=[M8])
            k.op("dve", lambda e: e.tensor_scalar(out=wk[:, 0:NK], in0=sc[:, 0:NK], scalar1=m8[:, 7:8], scalar2=NEGM, op0=ALU.is_lt, op1=ALU.mult),
                 reads=allsc + [M8], writes=[WK])
            for g in range(U // 4):
                ps, PS = p.ps_next(2, 4)
                for a in range(4):
                    u = g * 4 + a
                    k.op("pe", lambda e, ps=ps, a=a, u=u: e.transpose(out=ps[:, a * 128:(a + 1) * 128], in_=wk[:, u * 128:(u + 1) * 128], identity=p.identf[:]),
                         reads=[WK, p.C], writes=[PS])
                k.op("act", lambda e, ps=ps, g=g, qb=qb: e.activation(out=nmT[:, g * 4:(g + 1) * 4, qb * 128:(qb + 1) * 128],
                                                                      in_=ps[:].rearrange("p (a t) -> p a t", t=128), func=AF.Copy), reads=[PS], writes=[NMT])
            yield

    def gen_dm(ch, U, NK):
        for h in range(6):
            kslot = st["ka"] % 2
            st["ka"] += 1
            kd = KA[:, kslot * 8192:(kslot + 1) * 8192].rearrange("p (m s) -> p m s", s=4096)
            vd = VA[:, kslot * 8192:(kslot + 1) * 8192].rearrange("p (u c) -> p u c", c=256)
            k.dma("pool", kd[:, :, 0:NK], KD[ch, 2 * h:2 * h + 2, :, 0:NK].rearrange("m p s -> p m s"), writes=[KAb[kslot]], sem="ka%d" % kslot)
            k.dma("pool", vd[:, 0:U, :], VD[ch, 0:U, :, h * 256:(h + 1) * 256].rearrange("u p c -> p u c"), writes=[VAb[kslot]], sem="va%d" % kslot)
            qh, QH, qhi = rot(qhs, "q")
            qv = qh[:].rearrange("p (c t) -> p c t", t=512)
            k.dma("pool", qv[:, 0:2, :], qd[ch, 2 * h:2 * h + 2].rearrange("c p t -> p c t"), writes=[QH], sem="qh%d" % qhi)
            for m in range(2):
                gh, GH = load_bias(2 * h + m, ch)

                def s_parts(u, m=m, kd=kd, qv=qv):
                    return [(kd[:, m, u * 128:(u + 1) * 128], qv[:, m, :])]

                def av_parts(u, vd=vd):
                    return [(cc, vd[:, u, cc * 128:(cc + 1) * 128]) for cc in range(2)]
                attn_loop(ch, U, s_parts, [KAb[kslot], QH], gh, GH, av_parts, [VAb[kslot]], 2)
                for cc in range(2):
                    k.op("dve", lambda e, cc=cc, m=m: e.tensor_tensor(out=om[:, m * 2 + cc, :], in0=p.ps[cc][0][:], in1=rden[:], op=ALU.mult),
                         reads=[p.ps[cc][1], RD], writes=[OM])
                yield
            ps, PS = p.ps[7]
            for cc in range(2):
                k.op("dve", lambda e, cc=cc: e.scalar_tensor_tensor(out=om[:, cc, :], in0=om[:, 2 + cc, :], scalar=sm[:, 0:1], in1=om[:, cc, :],
                                                                    op0=ALU.mult, op1=ALU.add), reads=[OM, SM], writes=[OM])
                sq, SQ, _ = p.stagef()
                k.op("act", lambda e, cc=cc, sq=sq: e.activation(out=sq[:], in_=om[:, cc, :], func=AF.Square), reads=[OM], writes=[SQ])
                k.pe_group([lambda e, cc=cc, sq=sq: e.matmul(ps[:], lhsT=p.onesf[:], rhs=sq[:], start=(cc == 0), stop=(cc == 1))], reads=[SQ, p.C], writes=[PS])
            k.op("dve", lambda e: e.tensor_scalar(out=rden[:], in0=ps[:], scalar1=1.0 / 256, scalar2=1e-5, op0=ALU.mult, op1=ALU.add), reads=[PS], writes=[RD])
            k.op("act", lambda e: e.activation(out=rden[:], in_=rden[:], func=AF.Sqrt), reads=[RD], writes=[RD])
            k.op("dve", lambda e: e.reciprocal(out=rden[:], in_=rden[:]), reads=[RD], writes=[RD])
            for cc in range(2):
                store_y(ch, 2 * h + cc, lambda e, stg, cc=cc: e.scalar_tensor_tensor(out=stg[:], in0=om[:, cc, :], scalar=gs[:, cc:cc + 1], in1=rden[:],
                                                                                      op0=ALU.mult, op1=ALU.mult), [OM, RD, GS])
            yield
        NBK = U // 2
        for h in range(10):
            kslot = st["ka"] % 2
            st["ka"] += 1
            km = KA[:, kslot * 8192:kslot * 8192 + 4096]
            vm = VA[:, kslot * 8192:kslot * 8192 + 4096].rearrange("p (u c) -> p u c", c=128)
            k.dma("pool", km[:, 0:NK], KM[ch, h, :, 0:NK], writes=[KAb[kslot]], sem="ka%d" % kslot)
            k.dma("pool", vm[:, 0:U, :], VM[ch, 0:U, :, h * 128:(h + 1) * 128].rearrange("u p c -> p u c"), writes=[VAb[kslot]], sem="va%d" % kslot)
            qh, QH, qhi = rot(qhs, "q")
            k.dma("pool", qh[:, 0:512], qm[ch, h], writes=[QH], sem="qh%d" % qhi)
            gh, GH = load_bias(12 + h, ch)
            k.op("dve", lambda e, km=km: e.tensor_reduce(out=kmT[:, 0:NBK], in_=km[:, 0:NK].rearrange("p (v s) -> p v s", s=256), axis=AX.X, op=ALU.add),
                 reads=[KAb[kslot]], writes=[KMT])
            k.op("dve", lambda e: e.tensor_copy(out=kmTb[:, 0:NBK], in_=kmT[:, 0:NBK]), reads=[KMT], writes=[KMT])
            for qb in range(4):
                vown = 1 if qb < 2 else 0
                ps, PS = p.ps[7]
                k.pe_group([lambda e, ps=ps, qb=qb, qh=qh: e.matmul(ps[:, 0:NBK], lhsT=qh[:, qb * 128:(qb + 1) * 128], rhs=kmTb[:, 0:NBK], start=True, stop=True)],
                           reads=[QH, KMT], writes=[PS])
                k.op("dve", lambda e, ps=ps: e.tensor_tensor(out=gt[:, 0:NBK], in0=ps[:, 0:NBK], in1=tmB[:, ch * 16:ch * 16 + NBK], op=ALU.add),
                     reads=[PS, TMB], writes=[GT])
                if NBK < 16:
                    k.op("dve", lambda e: e.memset(gt[:, NBK:16], -1e30), writes=[GT])
                k.op("dve", lambda e, vown=vown: e.memset(gt[:, 0:vown + 1], -1e30), writes=[GT])
                k.op("dve", lambda e: e.max(out=m8b[:], in_=gt[:]), reads=[GT], writes=[M8B])
                k.op("dve", lambda e: e.tensor_scalar(out=m8b[:, 2:3], in0=m8b[:, 2:3], scalar1=-1e29, scalar2=None, op0=ALU.max), reads=[M8B], writes=[M8B])
                k.op("dve", lambda e: e.tensor_scalar(out=gt[:], in0=gt[:], scalar1=m8b[:, 2:3], scalar2=NEGM, op0=ALU.is_lt, op1=ALU.mult), reads=[GT, M8B], writes=[GT])
                k.op("dve", lambda e, vown=vown: e.memset(gt[:, vown:vown + 1], 0.0), writes=[GT])
                ps2, PS2 = p.ps[1]
                k.op("pe", lambda e, ps2=ps2: e.transpose(out=ps2[0:16, 0:128], in_=gt[:], identity=p.identf[:]), reads=[GT, p.C], writes=[PS2])
                k.op("act", lambda e, ps2=ps2, qb=qb: e.activation(out=snT[:, qb * 128:(qb + 1) * 128], in_=ps2[0:16, 0:128], func=AF.Copy), reads=[PS2], writes=[SNT])

            def s_parts(u, km=km, qh=qh):
                return [(km[:, u * 128:(u + 1) * 128], qh[:, 0:512]), (Es[:, (u // 2) * 128:(u // 2 + 1) * 128], snT[:, :])]

            def av_parts(u, vm=vm):
                return [(0, vm[:, u, :])]
            yield
            attn_loop(ch, U, s_parts, [KAb[kslot], QH, SNT, ESB], gh, GH, av_parts, [VAb[kslot]], 1)
            store_y(ch, 12 + h, lambda e, stg: e.tensor_tensor(out=stg[:], in0=p.ps[0][0][:], in1=rden[:], op=ALU.mult), [p.ps[0][1], RD])
            yield

    def dsa_attn(ch, U, NK):
        kvT = KA[:].rearrange("p (c s) -> p c s", s=4096)
        kvv = VA[:].rearrange("p (u c) -> p u c", c=512)
        k.dma("pool", kvT[:, :, 0:NK], KVT[ch, :, :, 0:NK].rearrange("c p s -> p c s"), writes=KAb, sem="ka")
        k.dma("pool", kvv[:, 0:U, :], KV[ch, 0:U].rearrange("u p c -> p u c"), writes=VAb, sem="va")
        for h in range(10):
            gh, GH = load_bias(22 + h, ch)
            qh, QH, qhi = rot(qhs, "q")
            qv = qh[:].rearrange("p (c t) -> p c t", t=512)
            k.dma("pool", qv, ql[ch, h * 4:(h + 1) * 4].rearrange("c p t -> p c t"), writes=[QH], sem="qh%d" % qhi)

            def s_parts(u, qv=qv):
                return [(kvT[:, cc, u * 128:(u + 1) * 128], qv[:, cc, :]) for cc in range(4)] + [(p.identb[:], nmT[:, u, :])]

            def av_parts(u):
                return [(cc, kvv[:, u, cc * 128:(cc + 1) * 128]) for cc in range(4)]
            attn_loop(ch, U, s_parts, KAb + [QH, NMT], gh, GH, av_parts, VAb, 4)
            for cc in range(4):
                k.op("act", lambda e, cc=cc: e.activation(out=olb[:, cc, :], in_=p.ps[cc][0][:], func=AF.Copy), reads=[p.ps[cc][1]], writes=[OLB])
            ps, PS = p.ps[7]
            k.pe_group([(lambda e, cc=cc, h=h: e.matmul(ps[:], lhsT=wuv[:, h * 4 + cc, :], rhs=olb[:, cc, :], start=(cc == 0), stop=(cc == 3)))
                        for cc in range(4)], reads=[WUV, OLB], writes=[PS])
            store_y(ch, 22 + h, lambda e, stg, ps=ps: e.tensor_tensor(out=stg[:], in0=ps[:], in1=rden[:], op=ALU.mult), [PS, RD])

    for ch in range(2):
        U = UCH[ch]
        NK = U * 128
        gp = gen_prep(ch, U, NK)
        gd = gen_dm(ch, U, NK)
        n_prep = 4 * ((U // 4) * 4 + 8 + 1)
        n_dm = 6 * 3 + 10 * 2
        ratio = max(1, n_prep // n_dm)
        alive_p, alive_d = True, True
        while alive_p or alive_d:
            for _ in range(ratio):
                if alive_p:
                    try:
                        next(gp)
                    except StopIteration:
                        alive_p = False
            if alive_d:
                try:
                    next(gd)
                except StopIteration:
                    alive_d = False
        dsa_attn(ch, U, NK)
    if own:
        k.finish()
    return p


def build_BC(layer_idx, with_proj):
    p = Prog()
    yT_int = p.nc.dram_tensor("yT_int", [2, 32, 128, 512], BF16).ap()
    build_B(layer_idx, p, yT_int)
    p.next_phase()
    build_C(with_proj, p, yT_int)
    p.k.finish()
    return p


def _t5_bucket_np(d):
    n = np.maximum(d, 0)
    nf = np.maximum(n, 1).astype(np.float32)
    large = 16 + (np.log(nf / np.float32(16)) / np.float32(math.log(8.0)) * np.float32(16)).astype(np.int32)
    large = np.minimum(large, 31)
    return np.where(n < 16, n, large)


def _bias_layout(rel_bias):
    s = np.arange(128)[:, None]
    w = np.arange(1024)[None, :]
    d = w - 384 - s
    idx = _t5_bucket_np(d)
    g = rel_bias[idx]
    g = np.where((d >= 0)[:, :, None], g, np.float32(NEGM))
    return np.ascontiguousarray(g.transpose(2, 0, 1)).astype(np.float32)


def _gather_full(outs, name, b, axis_tok, tok_per_chunk):
    parts = []
    for Q in range(8):
        r = Q if Q < 4 else 7 - Q
        li = 0 if Q < 4 else 1
        parts.append(np.asarray(outs[b * 4 + r][name][li]))
    return np.concatenate(parts, axis=axis_tok)


def _rel_tiles(full, Q, axis, tile):
    out = np.zeros_like(np.take(full, np.arange(32 * tile), axis=axis))
    for u in range(32):
        j = 4 * Q + 3 - u
        if j < 0:
            break
        src = [slice(None)] * full.ndim
        dst = [slice(None)] * full.ndim
        src[axis] = slice(j * tile, (j + 1) * tile)
        dst[axis] = slice(u * tile, (u + 1) * tile)
        out[tuple(dst)] = full[tuple(src)]
    return out


def run_B(l, outs, rel_bias, diff_lambda, diff_subln_g, w_uv):
    p = _prog("B", build_B, l)
    return _run(p, maps_B(l, outs, rel_bias, diff_lambda, diff_subln_g, w_uv))


def maps_B(l, outs, rel_bias, diff_lambda, diff_subln_g, w_uv):
    GB = _bias_layout(rel_bias)
    c31 = np.ascontiguousarray(np.broadcast_to(rel_bias[31][None, :], (128, 32))).astype(np.float32)
    lamv = np.ascontiguousarray(np.broadcast_to(diff_lambda[l].reshape(1, 512), (128, 512))).astype(np.float32)
    gsub = _cols(diff_subln_g[l])
    Esel = np.zeros((16, 2048), np.float32)
    for n in range(16):
        Esel[n, n * 128:(n + 1) * 128] = 1.0
    CM = np.where(np.arange(128)[None, :] > np.arange(128)[:, None], np.float32(-1e30), np.float32(0)).astype(np.float32)
    full = {}
    for b in range(2):
        full[b] = {
            "KD": _gather_full(outs, "kd", b, 2, 512),
            "VD": _gather_full(outs, "vd", b, 0, 4),
            "KM": _gather_full(outs, "km", b, 2, 512),
            "VM": _gather_full(outs, "vm", b, 0, 4),
            "KVT": _gather_full(outs, "kvT", b, 2, 512),
            "KV": _gather_full(outs, "kv", b, 0, 4),
            "IKT": _gather_full(outs, "ikT", b, 1, 512),
        }
    maps = []
    for c in range(8):
        b, qs = _chunks(c)
        f = full[b]
        m = {"b_" + n: np.asarray(outs[c][n]) for n in ("qd", "qm", "ql", "qi", "iw")}
        m["KD"] = np.stack([_rel_tiles(f["KD"], Q, 2, 128) for Q in qs])
        m["KM"] = np.stack([_rel_tiles(f["KM"], Q, 2, 128) for Q in qs])
        m["KVT"] = np.stack([_rel_tiles(f["KVT"], Q, 2, 128) for Q in qs])
        m["VD"] = np.stack([_rel_tiles(f["VD"], Q, 0, 1) for Q in qs])
        m["VM"] = np.stack([_rel_tiles(f["VM"], Q, 0, 1) for Q in qs])
        m["KV"] = np.stack([_rel_tiles(f["KV"], Q, 0, 1) for Q in qs])
        ik = [_rel_tiles(f["IKT"], Q, 1, 128) for Q in qs]
        m["IKT"] = np.stack([np.concatenate([a, a], axis=0) for a in ik])
        tm = np.zeros((128, 64), np.float32)
        tmI = np.zeros((128, 64), np.float32)
        tmB = np.zeros((128, 32), np.float32)
        for ch, Q in enumerate(qs):
            for u in range(32):
                if 4 * Q + 3 - u < 0:
                    tm[:, ch * 32 + u] = -30000.0
                    tmI[:, ch * 32 + u] = -1e30
            for v in range(16):
                if 2 * Q + 1 - v < 0:
                    tmB[:, ch * 16 + v] = -1e30
        m.update({"GB": GB, "c31": c31, "tm": tm, "tmI": tmI, "tmB": tmB, "lamv": lamv, "gsub": gsub,
                  "w_uv": np.ascontiguousarray(w_uv[l].reshape(10 * 512, 128)), "Esel": Esel, "CM": CM})
        m.update(_consts())
        maps.append(m)
    return maps


def run_C(l, outsB, res_prev, w_o, ln1_g, ln1_b, w_up, w_down, ln2_g, ln2_b, w_in, w_uk, kv_norm_g):
    with_proj = (l + 1 < DEPTH)
    p = _prog("C", build_C, with_proj)
    maps = []
    for c in range(8):
        m = {"yT": np.asarray(outsB[c]["yT"]), "res_in": np.asarray(res_prev[c]["res"]),
             "w_o": w_o[l], "w_up": w_up[l], "w_down": w_down[l],
             "ln1_g": _cols(ln1_g[l]), "ln1_b": _cols(ln1_b[l]), "ln2_g": _cols(ln2_g[l]), "ln2_b": _cols(ln2_b[l])}
        m.update(_consts())
        if with_proj:
            m.update(_proj_inputs(l + 1, w_in, w_uk, kv_norm_g))
        maps.append(m)
    return _run(p, maps)


def run_BC(l, prev, a):
    with_proj = (l + 1 < DEPTH)
    p = _prog("BC", build_BC, l, with_proj)
    maps = maps_B(l, prev, a["rel_bias"], a["diff_lambda"], a["diff_subln_g"], a["w_uv"])
    for c in range(8):
        m = maps[c]
        m.update({"res_in": np.asarray(prev[c]["res"]), "w_o": a["w_o"][l], "w_up": a["w_up"][l], "w_down": a["w_down"][l],
                  "ln1_g": _cols(a["ln1_g"][l]), "ln1_b": _cols(a["ln1_b"][l]),
                  "ln2_g": _cols(a["ln2_g"][l]), "ln2_b": _cols(a["ln2_b"][l])})
        if with_proj:
            m.update(_proj_inputs(l + 1, a["w_in"], a["w_uk"], a["kv_norm_g"]))
    return _run(p, maps)


def kernel(x, ln_emb_g, ln_emb_b, rel_bias, w_in, diff_lambda, diff_subln_g, kv_norm_g,
           w_uk, w_uv, w_o, ln1_g, ln1_b, w_up, w_down, ln2_g, ln2_b):
    a = {k_: np.asarray(v) for k_, v in dict(
        x=x, ln_emb_g=ln_emb_g, ln_emb_b=ln_emb_b, rel_bias=rel_bias, w_in=w_in, diff_lambda=diff_lambda,
        diff_subln_g=diff_subln_g, kv_norm_g=kv_norm_g, w_uk=w_uk, w_uv=w_uv, w_o=w_o, ln1_g=ln1_g, ln1_b=ln1_b,
        w_up=w_up, w_down=w_down, ln2_g=ln2_g, ln2_b=ln2_b).items()}
    cur = run_A(a["x"], a["ln_emb_g"], a["ln_emb_b"], a["w_in"], a["w_uk"], a["kv_norm_g"])
    for l in range(DEPTH):
        cur = run_BC(l, cur, a)
    out = np.zeros((NB, T, D), np.float32)
    for c in range(8):
        b, qs = _chunks(c)
        r = np.asarray(cur[c]["res"])
        for li, Q in enumerate(qs):
            out[b, Q * 512:(Q + 1) * 512, :] = r[li].reshape(D, 512).T
    return out
```
